# Optimizing a Trainium2 kernel written in Bass

```python
import jax, jax.numpy as jnp
from jax import lax
import numpy as np

D_MODEL = 1024
BATCH = 8
SEQ = 2048
DEPTH = 1
DEC_BATCH = 128
DEC_SEQ = 4
PAST_LEN = 16384
PAGE_SIZE = 128

N_META = 16
MIX_WIDTH = D_MODEL
C_CONV = MIX_WIDTH // 2
C_POOL = MIX_WIDTH - C_CONV
CONV_HEADS = 8
CONV_WIDTH = 31
CONV_HIST = CONV_WIDTH - 1
POOL_WINDOWS = (2, 4, 8, 16)
N_POOL_GROUPS = len(POOL_WINDOWS)
POOL_GROUP = C_POOL // N_POOL_GROUPS
POOL_HIST = max(POOL_WINDOWS) - 1
D_IN = 2 * C_CONV + C_POOL
D_FF = ((8 * D_MODEL // 3 + 127) // 128) * 128
EPS = 1e-6

kernel_name = "hymba_conv_pool_macaron_step"


def rmsnorm(x, g):
    xf = x.astype(jnp.float32)
    y = xf * lax.rsqrt(jnp.mean(xf * xf, axis=-1, keepdims=True) + EPS)
    return (y * g.astype(jnp.float32)).astype(x.dtype)


def layernorm(x, g, b):
    xf = x.astype(jnp.float32)
    mu = jnp.mean(xf, axis=-1, keepdims=True)
    xc = xf - mu
    var = jnp.mean(xc * xc, axis=-1, keepdims=True)
    y = xc * lax.rsqrt(var + EPS) * g.astype(jnp.float32) + b.astype(jnp.float32)
    return y.astype(x.dtype)


def swiglu(x, wg, wu, wd):
    return (jax.nn.silu(x @ wg) * (x @ wu)) @ wd


def depthwise_causal_conv(u_ext, w, b):
    out = lax.conv_general_dilated(
        u_ext, w.astype(u_ext.dtype)[:, None, :], window_strides=(1,), padding="VALID",
        dimension_numbers=("NWC", "WIO", "NWC"), feature_group_count=u_ext.shape[-1])
    return out + b.astype(u_ext.dtype)


def multiscale_pool(p_ext, pos0, w_lin, scale):
    bsz, L, _ = p_ext.shape
    T = L - POOL_HIST
    pf = p_ext.astype(jnp.float32)
    cs = jnp.concatenate([jnp.zeros_like(pf[:, :1]), jnp.cumsum(pf, axis=1)], axis=1)
    end = cs[:, POOL_HIST + 1:]
    pos = pos0 + jnp.arange(T)
    means = []
    for g, w in enumerate(POOL_WINDOWS):
        sl = slice(g * POOL_GROUP, (g + 1) * POOL_GROUP)
        start = cs[:, POOL_HIST + 1 - w: POOL_HIST + 1 - w + T, sl]
        cnt = jnp.minimum(pos + 1, w).astype(jnp.float32)[None, :, None]
        means.append((end[..., sl] - start) / cnt)
    d = jnp.concatenate(means, axis=-1) - pf[:, POOL_HIST:]
    d = d.reshape(bsz, T, N_POOL_GROUPS, POOL_GROUP)
    y = jnp.einsum("btgc,gcd->btgd", d, w_lin.astype(jnp.float32)).reshape(bsz, T, C_POOL)
    return (y * scale.astype(jnp.float32)).astype(p_ext.dtype)


def hybrid_layer(x, conv_prev, pool_prev, pos0,
                 g1, w1g, w1u, w1d, gm, w_in, b_in, w_dw, b_dw, ln_g, ln_b,
                 w_pool, pool_scale, w_out, b_out, g2, w2g, w2u, w2d):
    x = x + 0.5 * swiglu(rmsnorm(x, g1), w1g, w1u, w1d)
    h = rmsnorm(x, gm)
    z = h @ w_in + b_in
    u = z[..., :C_CONV] * jax.nn.sigmoid(z[..., C_CONV:2 * C_CONV])
    p_in = z[..., 2 * C_CONV:]
    u_ext = jnp.concatenate([conv_prev.astype(u.dtype), u], axis=1)
    c = depthwise_causal_conv(u_ext, w_dw, b_dw)
    c = jax.nn.silu(layernorm(c, ln_g, ln_b))
    p_ext = jnp.concatenate([pool_prev.astype(p_in.dtype), p_in], axis=1)
    q = multiscale_pool(p_ext, pos0, w_pool, pool_scale)
    x = x + jnp.concatenate([c, q], axis=-1) @ w_out + b_out
    x = x + 0.5 * swiglu(rmsnorm(x, g2), w2g, w2u, w2d)
    return x, u_ext[:, -CONV_HIST:], p_ext[:, -POOL_HIST:]


def setup_inputs(seed: int = 0) -> dict:
    key = jax.random.key(seed)
    ks = jax.random.split(key, 24)
    n = jax.random.normal
    f = jnp.float32

    def gain(k, shape):
        return 1.0 + 0.05 * n(k, shape, f)

    return {
        "x_prompt": n(ks[0], (BATCH, SEQ, D_MODEL), f),
        "x_sample": n(ks[1], (DEC_BATCH, DEC_SEQ, D_MODEL), f),
        "state_conv": 0.5 * n(ks[2], (DEPTH, DEC_BATCH, CONV_HIST, C_CONV), f),
        "state_pool": n(ks[3], (DEPTH, DEC_BATCH, POOL_HIST, C_POOL), f),
        "meta_tokens": n(ks[4], (N_META, D_MODEL), f),
        "norm_ffn1": gain(ks[5], (DEPTH, D_MODEL)),
        "w_ffn1_gate": n(ks[6], (DEPTH, D_MODEL, D_FF), f) * D_MODEL ** -0.5,
        "w_ffn1_up": n(ks[7], (DEPTH, D_MODEL, D_FF), f) * D_MODEL ** -0.5,
        "w_ffn1_down": n(ks[8], (DEPTH, D_FF, D_MODEL), f) * D_FF ** -0.5,
        "norm_mix": gain(ks[9], (DEPTH, D_MODEL)),
        "w_in": n(ks[10], (DEPTH, D_MODEL, D_IN), f) * D_MODEL ** -0.5,
        "b_in": 0.02 * n(ks[11], (DEPTH, D_IN), f),
        "w_dw": n(ks[12], (DEPTH, CONV_WIDTH, C_CONV), f) * CONV_WIDTH ** -0.5,
        "b_dw": 0.02 * n(ks[13], (DEPTH, C_CONV), f),
        "ln_conv_g": gain(ks[14], (DEPTH, C_CONV)),
        "ln_conv_b": 0.02 * n(ks[15], (DEPTH, C_CONV), f),
        "w_pool": n(ks[16], (DEPTH, N_POOL_GROUPS, POOL_GROUP, POOL_GROUP), f) * POOL_GROUP ** -0.5,
        "pool_scale": gain(ks[17], (DEPTH, C_POOL)),
        "w_out": n(ks[18], (DEPTH, MIX_WIDTH, D_MODEL), f) * MIX_WIDTH ** -0.5,
        "b_out": 0.02 * n(ks[19], (DEPTH, D_MODEL), f),
        "norm_ffn2": gain(ks[20], (DEPTH, D_MODEL)),
        "w_ffn2_gate": n(ks[21], (DEPTH, D_MODEL, D_FF), f) * D_MODEL ** -0.5,
        "w_ffn2_up": n(ks[22], (DEPTH, D_MODEL, D_FF), f) * D_MODEL ** -0.5,
        "w_ffn2_down": n(ks[23], (DEPTH, D_FF, D_MODEL), f) * D_FF ** -0.5,
        "norm_final": gain(jax.random.fold_in(key, 99), (D_MODEL,)),
    }


def reference(x_prompt, x_sample, state_conv, state_pool, meta_tokens,
              norm_ffn1, w_ffn1_gate, w_ffn1_up, w_ffn1_down,
              norm_mix, w_in, b_in, w_dw, b_dw, ln_conv_g, ln_conv_b,
              w_pool, pool_scale, w_out, b_out,
              norm_ffn2, w_ffn2_gate, w_ffn2_up, w_ffn2_down, norm_final):
    bsz = x_prompt.shape[0]
    meta = jnp.broadcast_to(meta_tokens.astype(x_prompt.dtype)[None], (bsz, N_META, D_MODEL))
    xp = jnp.concatenate([meta, x_prompt], axis=1)
    xs = x_sample
    conv_p, pool_p, conv_s, pool_s = [], [], [], []
    for l in range(DEPTH):
        params = (norm_ffn1[l], w_ffn1_gate[l], w_ffn1_up[l], w_ffn1_down[l],
                  norm_mix[l], w_in[l], b_in[l], w_dw[l], b_dw[l], ln_conv_g[l], ln_conv_b[l],
                  w_pool[l], pool_scale[l], w_out[l], b_out[l],
                  norm_ffn2[l], w_ffn2_gate[l], w_ffn2_up[l], w_ffn2_down[l])
        zc = jnp.zeros((bsz, CONV_HIST, C_CONV), xp.dtype)
        zp = jnp.zeros((bsz, POOL_HIST, C_POOL), xp.dtype)
        xp, cpn, ppn = hybrid_layer(xp, zc, zp, 0, *params)
        xs, csn, psn = hybrid_layer(xs, state_conv[l], state_pool[l], PAST_LEN, *params)
        conv_p.append(cpn)
        pool_p.append(ppn)
        conv_s.append(csn)
        pool_s.append(psn)
    y_prompt = rmsnorm(xp, norm_final)[:, N_META:]
    y_sample = rmsnorm(xs, norm_final)
    new_conv_prompt = jnp.stack(conv_p, axis=0)
    new_pool_prompt = jnp.stack(pool_p, axis=0)
    new_conv_sample = jnp.stack(conv_s, axis=0)
    new_pool_sample = jnp.stack(pool_s, axis=0)
    return (y_prompt, y_sample, new_conv_prompt, new_pool_prompt, new_conv_sample, new_pool_sample)
```

```python
import numpy as np
import concourse.bass as bass
import concourse.mybir as mybir
from concourse.bass_utils import run_bass_kernel_spmd

F32 = mybir.dt.float32
BF16 = mybir.dt.bfloat16
AF = mybir.ActivationFunctionType
ALU = mybir.AluOpType

D = 1024
KC = 8
DFF = 2816
JC = 22
CC = 512
NMETA = 16
SEQ = 2048
PT = NMETA + SEQ
NSS = 16
DSEQ = 4
ST = NSS * DSEQ
T = PT + ST
CONV_W = 31
CH = 30
PH = 15
WINS = (2, 4, 8, 16)
EPS = 1e-6
TILES = [(0, 432), (432, 432), (864, 432), (1296, 432), (1728, 400)]
NMAX = 432

C_G1, C_GM, C_G2, C_GF, C_BIN, C_BDW, C_LNG, C_LNB, C_PSC, C_BOUT, C_WDW = 0, 8, 16, 24, 32, 44, 48, 52, 56, 60, 68
NCONST = C_WDW + CONV_W * 4

N_GU = 6
N_DS = 4
WEIGHT_MODE = "stream32"


class Res:
    __slots__ = ("w", "r", "name")

    def __init__(self, name=""):
        self.w = None
        self.r = []
        self.name = name


class Prog:
    QUEUES = ("pe", "act", "dve", "pool", "sp")

    def __init__(self):
        self.q = {k: [] for k in self.QUEUES}
        self.cnt = {k: 0 for k in self.QUEUES}
        self.waited = {k: {} for k in self.QUEUES}
        self.dma_sems = []

    def new_dma_sem(self, name):
        self.cnt[name] = 0
        self.dma_sems.append(name)
        return name

    def _waits(self, q, deps):
        out = []
        for s, v in deps.items():
            if q == "pe" and s == "pe":
                continue
            if self.waited[q].get(s, 0) >= v:
                continue
            self.waited[q][s] = v
            out.append((s, v))
        return out

    @staticmethod
    def _flat(rs):
        out = []
        for r in rs:
            if isinstance(r, (list, tuple)):
                out.extend(Prog._flat(r))
            else:
                out.append(r)
        return out

    @staticmethod
    def _add(deps, tok):
        if tok is None:
            return
        s, v = tok
        if deps.get(s, 0) < v:
            deps[s] = v

    def op(self, q, fn, reads=(), writes=()):
        reads = self._flat(reads)
        writes = self._flat(writes)
        deps = {}
        for r in reads:
            self._add(deps, r.w)
        for r in writes:
            self._add(deps, r.w)
            for t in r.r:
                self._add(deps, t)
        waits = self._waits(q, deps)
        self.cnt[q] += 1
        tok = (q, self.cnt[q])
        self.q[q].append((waits, fn, True))
        for r in reads:
            r.r.append(tok)
        for r in writes:
            r.w = tok
            r.r = []
        return tok

    def pe_group(self, mms, writes):
        n = len(mms)
        allreads = []
        writes = self._flat(writes)
        for i, (fn, reads) in enumerate(mms):
            reads = self._flat(reads)
            deps = {}
            for r in reads:
                self._add(deps, r.w)
            if i == 0:
                for r in writes:
                    self._add(deps, r.w)
                    for t in r.r:
                        self._add(deps, t)
            waits = self._waits("pe", deps)
            self.q["pe"].append((waits, fn, i == n - 1))
            allreads.extend(reads)
        self.cnt["pe"] += 1
        tok = ("pe", self.cnt["pe"])
        seen = set()
        for r in allreads:
            if id(r) in seen:
                continue
            seen.add(id(r))
            r.r.append(tok)
        for r in writes:
            r.w = tok
            r.r = []
        return tok

    def dma(self, q, fns, sem, reads=(), writes=()):
        reads = self._flat(reads)
        writes = self._flat(writes)
        deps = {}
        for r in reads:
            self._add(deps, r.w)
        for r in writes:
            self._add(deps, r.w)
            for t in r.r:
                self._add(deps, t)
        waits = self._waits(q, deps)
        for i, fn in enumerate(fns):
            self.q[q].append((waits if i == 0 else [], (fn, sem), None))
            self.cnt[sem] += 16
        tok = (sem, self.cnt[sem])
        for r in reads:
            r.r.append(tok)
        for r in writes:
            r.w = tok
            r.r = []
        return tok

    def replay(self, q, eng, sems):
        for waits, fn, inc in self.q[q]:
            for s, v in waits:
                eng.wait_ge(sems[s], v)
            if inc is None:
                f, sem = fn
                f(eng, sems[sem])
            else:
                ins = fn(eng)
                if inc:
                    ins.then_inc(sems[q], 1)


class Ring:
    def __init__(self, P, name, nslots, tensors, loads):
        self.P = P
        self.n = nslots
        self.t = tensors
        self.res = [Res(f"{name}{i}") for i in range(nslots)]
        self.sems = {"sp": [P.new_dma_sem(f"{name}{i}") for i in range(nslots)],
                     "pool": [P.new_dma_sem(f"{name}c{i}") for i in range(nslots)]}
        self.loads = loads
        self.emitted = 0
        self.consumed = 0
        self.held = 0

    def refill(self):
        lim = min(len(self.loads), self.consumed - self.held + self.n)
        while self.emitted < lim:
            i = self.emitted
            s = i % self.n
            queue, ld, rd, post = self.loads[i]
            dst = self.t[s]
            self.P.dma(queue, [lambda e, sm, ld=ld, dst=dst: ld(e, sm, dst)], self.sems[queue][s], reads=rd, writes=[self.res[s]])
            if post is not None:
                post(dst, self.res[s])
            self.emitted += 1

    def get(self):
        self.refill()
        i = self.consumed
        assert i < self.emitted, "ring underflow"
        self.consumed += 1
        s = i % self.n
        return self.t[s], self.res[s]


def build_program():
    nc = bass.Bass("TRN2", target_bir_lowering=False)
    P = Prog()

    def din(name, shape):
        return nc.dram_tensor(name, list(shape), F32, kind="ExternalInput").ap()

    def dout(name, shape):
        return nc.dram_tensor(name, list(shape), F32, kind="ExternalOutput").ap()

    xp = din("xp", (SEQ, D))
    xs = din("xs", (ST, D))
    meta = din("meta", (NMETA, D))
    sconv = din("sconv", (NSS * CH, CC))
    spool = din("spool", (NSS * PH, CC))
    wgu = [din("wgu1", (JC, 128, 2 * KC * 128)), din("wgu2", (JC, 128, 2 * KC * 128))]
    wd = [din("wd1", (KC, 128, JC * 128)), din("wd2", (KC, 128, JC * 128))]
    win = din("win", (6, 128, 2 * KC * 128))
    wout = din("wout", (4, 128, 2 * KC * 128))
    wpool = din("wpool", (128, 4 * 128))
    consts_d = din("consts", (128, NCONST))
    ident_d = din("ident", (128, 128))
    invc_d = din("invc", (128, 4 * 16))
    gfbc_d = din("gfbc", (128, D))

    yp = dout("yp", (SEQ, D))
    ys = dout("ys", (ST, D))
    ncp = dout("ncp", (CH, CC))
    npp = dout("npp", (PH, CC))
    ncs = dout("ncs", (NSS * CH, CC))
    nps = dout("nps", (NSS * PH, CC))

    import os
    DBG = int(os.environ.get("KDBG", "-1"))
    dbg_d = [dout(f"dbg{i}", (128, KC * NMAX)) for i in range(5)] if DBG >= 0 else None
    from contextlib import ExitStack
    es = ExitStack()

    def sb(name, shape, dt=F32):
        return es.enter_context(nc.sbuf_tensor("sb_" + name, list(shape), dt))

    def pst(name):
        return es.enter_context(nc.psum_tensor(name, [128, 512], F32))

    with es:
        consts = sb("consts", (128, NCONST))
        hb = sb("hb", (128, 8))
        ident32 = sb("ident32", (128, 128))
        identb = sb("identb", (128, 128), BF16)
        ones_rms = sb("ones_rms", (128, 128), BF16)
        ones_ln = sb("ones_ln", (128, 128), BF16)
        invc = sb("invc", (128, 4 * 16))
        gf_bc = sb("gf_bc", (128, D))
        ones32 = sb("ones32", (128, 8))
        rcol = [sb(f"rcol{i}", (128, 8)) for i in range(2)]
        mhalf = sb("mhalf", (128, 8))
        dummy = sb("dummy", (128, 8))
        epsc = mhalf
        diag = sb("diag", (128, CONV_W * 4, 128), BF16)
        wpool_b = sb("wpool_b", (128, 4, 128), BF16)

        xT = sb("xT", (128, KC, NMAX))
        xn = sb("xn", (128, KC, NMAX), BF16)
        hbuf = sb("hbuf", (128, JC, NMAX), BF16)
        gu_t = [sb(f"gu{i}", (128, 2, KC, 128), BF16) for i in range(N_GU)]
        ds_t = [sb(f"ds{i}", (128, JC, 128), BF16) for i in range(N_DS)]
        sg_t = [sb(f"sg{i}", (128, NMAX)) for i in range(3)]
        sq_t = [sb(f"sq{i}", (128, NMAX), BF16) for i in range(8)]
        sst = sb("sst", (128, NMAX))
        rstd = sb("rstd", (128, NMAX))
        stg_in = [sb(f"stgi{i}", (128, D)) for i in range(2)]
        stg_out = [sb(f"stgo{i}", (128, D)) for i in range(2)]
        th_t = [sb(f"th{i}", (128, NMAX)) for i in range(2)]
        ah_t = [sb(f"ah{i}", (128, NMAX)) for i in range(2)]
        u32_t = [sb(f"u32{i}", (128, NMAX)) for i in range(2)]
        ubf = sb("ubf", (128, 4, CH + NMAX), BF16)
        usbf = sb("usbf", (128, 4, NSS, CH + DSEQ), BF16)
        utail = sb("utail", (128, 4, CH))
        usnew = sb("usnew", (128, 4, ST))
        c32 = hbuf[:, 0:8, :].rearrange("p j n -> p (j n)").bitcast(F32).rearrange("p (c n) -> p c n", c=4)
        cbf_t = [sb(f"cbf{i}", (128, NMAX), BF16) for i in range(2)]
        csq_t = [sb(f"csq{i}", (128, NMAX), BF16) for i in range(2)]
        ln_mean = hbuf[:, 16:18, :].rearrange("p j n -> p (j n)").bitcast(F32)
        ln_var = hbuf[:, 18:20, :].rearrange("p j n -> p (j n)").bitcast(F32)
        ln_rstd = sb("ln_rstd", (128, NMAX))
        ln_mr = sb("ln_mr", (128, NMAX))
        cq = hbuf[:, 8:16, :]
        pext = sb("pext", (128, 4, PH + NMAX))
        psext = sb("psext", (128, 4, NSS, PH + DSEQ))
        sA = sb("sA", (128, PH + NMAX))
        sB = sb("sB", (128, PH + NMAX))
        sAs = sb("sAs", (128, NSS, PH + DSEQ))
        sBs = sb("sBs", (128, NSS, PH + DSEQ))
        dd_t = [sb(f"dd{i}", (128, NMAX), BF16) for i in range(4)]
        ptail = sb("ptail", (128, 4, PH))
        psnew = sb("psnew", (128, 4, ST))
        stg_st = sb("stg_st", (128, CC))
        wpool_f = stg_st
        stg_so = [stg_out[i][:, 0:CC] for i in range(2)]

        ps = [pst(f"ps{i}") for i in range(8)]

        R = Res
        r_consts, r_hb, r_ident32, r_identb, r_ones, r_invc, r_diag, r_wpf, r_wpb = (R() for _ in range(9))
        r_mhalf = R()
        r_dummy = R()
        r_rcol = [R(), R()]
        r_xT = [R(f"xT{c}") for c in range(KC)]
        r_xn = [R(f"xn{c}") for c in range(KC)]
        r_h = [R(f"h{j}") for j in range(JC)]
        r_sg = [R() for _ in range(3)]
        r_sq = [R() for _ in range(8)]
        r_sst, r_rstd, r_rscr = R(), R(), R()
        r_stgi = [R(), R()]
        r_stgo = [R(), R()]
        r_ps = [R(f"ps{i}") for i in range(8)]
        r_th = [R(), R()]
        r_ah = [R(), R()]
        r_u32 = [R(), R()]
        r_ubf = [R() for _ in range(4)]
        r_usbf = [R() for _ in range(4)]
        r_utail, r_usnew = R(), R()
        r_c32 = [[r_h[2 * c], r_h[2 * c + 1]] for c in range(4)]
        r_cbf = [R(), R()]
        r_csq = [R(), R()]
        r_lnm, r_lnv, r_lnr, r_lnmr = [r_h[16], r_h[17]], [r_h[18], r_h[19]], R(), R()
        r_cq = [r_h[8 + k] for k in range(8)]
        r_pext = [R() for _ in range(4)]
        r_psext = [R() for _ in range(4)]
        r_sA, r_sB, r_sAs, r_sBs = R(), R(), R(), R()
        r_dd = [R() for _ in range(4)]
        r_ptail, r_psnew = R(), R()
        r_stgst = R()
        r_wpf = r_stgst
        r_stgso = r_stgo

        sem_misc = P.new_dma_sem("misc")
        sem_stgi = [P.new_dma_sem("stgi0"), P.new_dma_sem("stgi1")]
        sem_stgo = [P.new_dma_sem("stgo0"), P.new_dma_sem("stgo1")]
        sem_st = P.new_dma_sem("st")
        sem_so = [P.new_dma_sem(f"so{i}") for i in range(2)]
        sem_d2d = P.new_dma_sem("d2d")
        sem_dbg = P.new_dma_sem("dbg")

        def dump(t, i):
            if DBG == t:
                P.dma("sp", [lambda e, sm, i=i: e.dma_start(out=dbg_d[i][:, :], in_=xT[:, :, :].rearrange("p c n -> p (c n)")).then_inc(sm, 16)],
                      sem_dbg, reads=r_xT)

        class RR:
            def __init__(self, items):
                self.items = items
                self.i = 0

            def next(self):
                v = self.items[self.i % len(self.items)]
                self.i += 1
                return v

        bank_g = RR([0, 1])
        bank_u = RR([2, 3])
        bank_d = RR([4, 5])
        bank_s = RR([6])
        bank_t = RR([7, 6])
        rr_sg = RR([0, 1, 2])
        rr_sq = RR(list(range(8)))
        rr_stgi = RR([0, 1])
        rr_stgo = RR([0, 1])
        rr2 = {k: RR([0, 1]) for k in ("th", "ah", "u32", "cbf", "csq", "dd")}
        rr_so = RR([0, 1])

        wgu_b = [nc.dram_tensor("wgu1_b", [JC, 128, 2 * KC * 128], BF16).ap(), nc.dram_tensor("wgu2_b", [JC, 128, 2 * KC * 128], BF16).ap()]
        wd_b = [nc.dram_tensor("wd1_b", [KC, 128, JC * 128], BF16).ap(), nc.dram_tensor("wd2_b", [KC, 128, JC * 128], BF16).ap()]
        win_b = nc.dram_tensor("win_b", [6, 128, 2 * KC * 128], BF16).ap()
        wout_b = nc.dram_tensor("wout_b", [4, 128, 2 * KC * 128], BF16).ap()
        NPC = 8
        pc_sems = [P.new_dma_sem(f"pc{i}") for i in range(NPC)]
        pc_slot_res = [Res() for _ in range(NPC)]
        pc_state = {"n": 0}
        chunk_res = {}

        def mk_loads(src32, srcb, i, kind, maxlast):
            key = (id(srcb), i)
            chunk_res[key] = Res()
            flat = (lambda dst: dst[:, :, :, :].rearrange("p a k n -> p (a k n)")) if kind == "gu" else \
                   (lambda dst: dst[:, :, :].rearrange("p j n -> p (j n)"))

            def ld0(e, sm, dst):
                e.dma_start(out=flat(dst), in_=src32[i, :, :], max_dma_last_dim=maxlast).then_inc(sm, 16)

            def post0(dst, slot_res):
                sl = pc_state["n"] % NPC
                pc_state["n"] += 1
                P.dma("sp", [lambda e, sm: e.dma_start(out=srcb[i, :, :], in_=flat(dst)).then_inc(sm, 16)], pc_sems[sl],
                      reads=[slot_res], writes=[chunk_res[key], pc_slot_res[sl]])

            def ld1(e, sm, dst):
                e.dma_start(out=flat(dst), in_=srcb[i, :, :]).then_inc(sm, 16)

            return ("pool", ld0, [], post0), ("sp", ld1, [chunk_res[key]], None), ("pool", ld0, [], None)

        gu_seq = ([(wgu[0], wgu_b[0], j) for j in range(JC)] + [(win, win_b, i) for i in (0, 1, 4, 2, 5, 3)] +
                  [(wout, wout_b, i) for i in range(4)] + [(wgu[1], wgu_b[1], j) for j in range(JC)])
        ds_seq = [(wd[0], wd_b[0], m) for m in range(KC)] + [(wd[1], wd_b[1], m) for m in range(KC)]
        gu_pairs = [mk_loads(a_, b_, i, "gu", 8192) for (a_, b_, i) in gu_seq]
        ds_pairs = [mk_loads(a_, b_, i, "ds", 5632) for (a_, b_, i) in ds_seq]
        def seq_loads(pairs):
            out = []
            out += [p[0] if i % 2 == 0 else p[2] for i, p in enumerate(pairs)]
            out += [p[1] if i % 2 == 0 else p[0] for i, p in enumerate(pairs)]
            for _t in range(2, len(TILES)):
                out += [p[1] for p in pairs]
            return out
        gu_loads = seq_loads(gu_pairs)
        ds_loads = seq_loads(ds_pairs)
        if WEIGHT_MODE == "stream32":
            gu_loads = [p[2] for p in gu_pairs] * len(TILES)
            ds_loads = [p[2] for p in ds_pairs] * len(TILES)
        ring_gu = Ring(P, "rgu", N_GU, gu_t, gu_loads)
        ring_ds = Ring(P, "rds", N_DS, ds_t, ds_loads)

        def cc(col, n=1):
            return consts[:, col:col + n]

        P.dma("sp", [lambda e, sm: e.dma_start(out=consts[:, :], in_=consts_d[:, :]).then_inc(sm, 16),
                     lambda e, sm: e.dma_start(out=ident32[:, :], in_=ident_d[:, :]).then_inc(sm, 16),
                     lambda e, sm: e.dma_start(out=invc[:, :], in_=invc_d[:, :]).then_inc(sm, 16),
                     lambda e, sm: e.dma_start(out=wpool_f[:, :], in_=wpool[:, :]).then_inc(sm, 16),
                     lambda e, sm: e.dma_start(out=gf_bc[:, :], in_=gfbc_d[:, :]).then_inc(sm, 16)],
              sem_misc, writes=[r_consts, r_ident32, r_invc, r_wpf])

        P.op("dve", lambda e: e.memset(mhalf[:, :], EPS), writes=[r_mhalf])
        P.op("dve", lambda e: e.memset(ones32[:, :], 1.0), writes=[r_ones])
        P.op("dve", lambda e: e.memset(ones_rms[:, :], 1.0 / D), writes=[r_ones])
        P.op("dve", lambda e: e.memset(ones_ln[:, :], 1.0 / CC), writes=[r_ones])
        P.op("dve", lambda e: e.tensor_scalar(out=hb[:, :], in0=consts[:, C_BIN:C_BIN + 8], scalar1=0.5, scalar2=None,
                                              op0=ALU.mult), reads=[r_consts], writes=[r_hb])
        P.op("dve", lambda e: e.tensor_copy(out=wpool_b[:, :, :].rearrange("p g n -> p (g n)"), in_=wpool_f[:, :]),
             reads=[r_wpf], writes=[r_wpb])
        P.op("pool", lambda e: e.memset(sA[:, :], 0.0), writes=[r_sA])
        P.op("pool", lambda e: e.memset(sB[:, :], 0.0), writes=[r_sB])
        P.op("pool", lambda e: e.memset(sAs[:, :, :], 0.0), writes=[r_sAs])
        P.op("pool", lambda e: e.memset(sBs[:, :, :], 0.0), writes=[r_sBs])
        P.op("dve", lambda e: e.memset(ubf[:, :, 0:CH], 0.0), writes=r_ubf)
        P.op("dve", lambda e: e.memset(pext[:, :, 0:PH], 0.0), writes=r_pext)
        diag_todo = [(k, c) for k in range(CONV_W) for c in range(4)]

        def diag_some(n):
            for _ in range(min(n, len(diag_todo))):
                k, c = diag_todo.pop(0)
                P.op("dve", lambda e, k=k, c=c: e.tensor_scalar(
                    out=diag[:, k * 4 + c, :], in0=ident32[:, :], scalar1=cc(C_WDW + k * 4 + c), scalar2=None,
                    op0=ALU.mult), reads=[r_ident32, r_consts], writes=[r_diag])

        def load_states():
          for blk in range(4):
              P.dma("sp", [lambda e, sm, blk=blk: e.dma_start(out=stg_st[0:120, :], in_=sconv[blk * 120:(blk + 1) * 120, :]
                                                               ).then_inc(sm, 16)], sem_st, writes=[r_stgst])
              b = bank_t.next()
              P.pe_group([(lambda e, c=c, b=b: e.transpose(out=ps[b][:, c * 128:c * 128 + 120],
                                                           in_=stg_st[0:120, c * 128:(c + 1) * 128],
                                                           identity=ident32[0:120, 0:120]), [r_stgst, r_ident32])
                          for c in range(4)], writes=[r_ps[b]])
              for c in range(4):
                  P.op("act", lambda e, c=c, b=b, blk=blk: e.activation(
                      out=usbf[:, c, blk * 4:(blk + 1) * 4, 0:CH],
                      in_=ps[b][:, c * 128:c * 128 + 120].rearrange("p (s r) -> p s r", r=CH), func=AF.Copy),
                      reads=[r_ps[b]], writes=[r_usbf[c]])
          for blk in range(2):
              P.dma("sp", [lambda e, sm, blk=blk: e.dma_start(out=stg_st[0:120, :], in_=spool[blk * 120:(blk + 1) * 120, :]
                                                               ).then_inc(sm, 16)], sem_st, writes=[r_stgst])
              b = bank_t.next()
              P.pe_group([(lambda e, c=c, b=b: e.transpose(out=ps[b][:, c * 128:c * 128 + 120],
                                                           in_=stg_st[0:120, c * 128:(c + 1) * 128],
                                                           identity=ident32[0:120, 0:120]), [r_stgst, r_ident32])
                          for c in range(4)], writes=[r_ps[b]])
              for c in range(4):
                  P.op("act", lambda e, c=c, b=b, blk=blk: e.activation(
                      out=psext[:, c, blk * 8:(blk + 1) * 8, 0:PH],
                      in_=ps[b][:, c * 128:c * 128 + 120].rearrange("p (s r) -> p s r", r=PH), func=AF.Copy),
                      reads=[r_ps[b]], writes=[r_psext[c]])

        P.dma("sp", [lambda e, sm: e.dma_start(
            out=ncs.rearrange("(s r) c -> s (r c)", r=CH)[:, 0:(CH - DSEQ) * CC],
            in_=sconv.rearrange("(s r) c -> s (r c)", r=CH)[:, DSEQ * CC:CH * CC]).then_inc(sm, 16),
            lambda e, sm: e.dma_start(
            out=nps.rearrange("(s r) c -> s (r c)", r=PH)[:, 0:(PH - DSEQ) * CC],
            in_=spool.rearrange("(s r) c -> s (r c)", r=PH)[:, DSEQ * CC:PH * CC]).then_inc(sm, 16)], sem_d2d)

        def segments(col, n):
            out = []
            c = col
            end = col + n
            while c < end:
                if c < NMETA:
                    e = min(end, NMETA)
                    out.append(("meta", c, c - col, e - c))
                elif c < PT:
                    e = min(end, PT)
                    out.append(("p", c - NMETA, c - col, e - c))
                else:
                    e = end
                    out.append(("s", c - PT, c - col, e - c))
                c = e
            return out

        in_src = {"meta": meta, "p": xp, "s": xs}
        out_dst = {"p": yp, "s": ys}

        stg_all = stg_in + stg_out
        r_stg_all = r_stgi + r_stgo
        sem_stg_all = sem_stgi + sem_stgo

        def load_x_dma_block(t, b):
            col0, N = TILES[t]
            rows = min(128, N - b * 128)
            si = b if t == 0 else rr_stgi.next()
            fns = []
            for kind, r0, poff, nr in segments(col0 + b * 128, rows):
                fns.append(lambda e, sm, kind=kind, r0=r0, poff=poff, nr=nr, si=si: e.dma_start(
                    out=stg_all[si][poff:poff + nr, :], in_=in_src[kind][r0:r0 + nr, :]).then_inc(sm, 16))
            P.dma("act" if (t == 0 or b >= 2) else "sp", fns, sem_stg_all[si], writes=[r_stg_all[si]])
            return (b, rows, si)

        def load_x_dma(t):
            col0, N = TILES[t]
            nblk = (N + 127) // 128
            return [load_x_dma_block(t, b) for b in range(min(4 if t == 0 else 2, nblk))]

        def load_x_transpose(t, blocks):
            col0, N = TILES[t]
            nblk = (N + 127) // 128
            blocks = list(blocks)
            bi = 0
            while bi < len(blocks):
                b, rows, si = blocks[bi]
                bi += 1
                for half in range(2):
                    bk = bank_t.next()
                    P.pe_group([(lambda e, cl=cl, bk=bk, rows=rows, si=si, half=half: e.transpose(
                        out=ps[bk][:, cl * 128:cl * 128 + rows],
                        in_=stg_all[si][0:rows, (half * 4 + cl) * 128:(half * 4 + cl + 1) * 128],
                        identity=ident32[0:rows, 0:rows]), [r_stg_all[si], r_ident32]) for cl in range(4)],
                        writes=[r_ps[bk]])
                    P.op("act", lambda e, bk=bk, rows=rows, half=half, b=b: e.activation(
                        out=xT[:, half * 4:(half + 1) * 4, b * 128:b * 128 + rows],
                        in_=ps[bk][:, :].rearrange("p (c n) -> p c n", n=128)[:, :, 0:rows], func=AF.Copy),
                        reads=[r_ps[bk]], writes=r_xT[half * 4:(half + 1) * 4])
                if len(blocks) < nblk:
                    blocks.append(load_x_dma_block(t, len(blocks)))

        def act_pre(func):
            P.op("act", lambda e: e.activation(out=dummy[:, 0:1], in_=epsc[:, 0:1], func=func), reads=[r_mhalf], writes=[r_dummy])

        def post_x(c, gcol, N):
            if gcol is not None:
                P.op("act", lambda e: e.activation(out=xn[:, c, 0:N], in_=xT[:, c, 0:N], func=AF.Identity, scale=cc(gcol + c)),
                     reads=[r_xT[c], r_consts], writes=[r_xn[c]])
            if c in (0, 1):
                P.op("act", lambda e: e.activation(out=sq_t[c][:, 0:N], in_=xT[:, c, 0:N], func=AF.Square),
                     reads=[r_xT[c]], writes=[r_sq[c]])
            elif c in (2, 3, 4):
                P.op("dve", lambda e: e.tensor_tensor(out=sq_t[c][:, 0:N], in0=xT[:, c, 0:N], in1=xT[:, c, 0:N], op=ALU.mult),
                     reads=[r_xT[c]], writes=[r_sq[c]])
            else:
                P.op("pool", lambda e: e.tensor_tensor(out=sq_t[c][:, 0:N], in0=xT[:, c, 0:N], in1=xT[:, c, 0:N], op=ALU.mult),
                     reads=[r_xT[c]], writes=[r_sq[c]])

        def stats_rstd(N):
            bk = bank_s.next()
            P.pe_group([(lambda e, c=c: e.matmul(ps[bk][:, 0:N], lhsT=ones_rms[:, :], rhs=sq_t[c][:, 0:N],
                                                 start=(c == 0), stop=(c == KC - 1)), [r_sq[c], r_ones]) for c in range(KC)],
                       writes=[r_ps[bk]])
            P.op("act", lambda e: e.activation(out=sst[:, 0:N], in_=ps[bk][:, 0:N], func=AF.Sqrt, bias=epsc[:, 0:1]),
                 reads=[r_ps[bk], r_mhalf], writes=[r_sst])
            act_pre(AF.Silu)
            P.op("dve", lambda e: e.reciprocal(out=rstd[:, 0:N], in_=sst[:, 0:N]), reads=[r_sst], writes=[r_rstd])

        def ffn(N, next_gcol):
            wt, wr = ring_gu.get()
            ring_gu.held = 1
            for j in range(JC):
                nxt = None
                if j + 1 < JC:
                    nxt = ring_gu.get()
                    ring_gu.held = 2
                bg = bank_g.next()
                bu = bank_u.next()
                P.pe_group([(lambda e, kc=kc, wt=wt, bg=bg: e.matmul(ps[bg][:, 0:N], lhsT=wt[:, 0, kc, :], rhs=xn[:, kc, 0:N],
                                                                     start=(kc == 0), stop=(kc == KC - 1)), [wr, r_xn[kc]])
                            for kc in range(KC)] +
                           [(lambda e, kc=kc, wt=wt, bu=bu: e.matmul(ps[bu][:, 0:N], lhsT=wt[:, 1, kc, :], rhs=xn[:, kc, 0:N],
                                                                     start=(kc == 0), stop=(kc == KC - 1)),
                             [wr, r_xn[kc]] + ([nxt[1]] if (nxt is not None and kc == 3) else []))
                            for kc in range(KC)], writes=[r_ps[bg], r_ps[bu]])
                ring_gu.held -= 1
                ring_gu.refill()
                si = rr_sg.next()
                P.op("dve", lambda e, si=si, bg=bg: e.tensor_tensor(out=sg_t[si][:, 0:N], in0=ps[bg][:, 0:N], in1=rstd[:, 0:N], op=ALU.mult),
                     reads=[r_ps[bg], r_rstd], writes=[r_sg[si]])
                P.op("act", lambda e, si=si: e.activation(out=sg_t[si][:, 0:N], in_=sg_t[si][:, 0:N], func=AF.Silu),
                     reads=[r_sg[si]], writes=[r_sg[si]])
                P.op("pool", lambda e, si=si: e.tensor_tensor(out=sg_t[si][:, 0:N], in0=sg_t[si][:, 0:N], in1=rstd[:, 0:N], op=ALU.mult),
                     reads=[r_sg[si], r_rstd], writes=[r_sg[si]])
                P.op("dve", lambda e, si=si, bu=bu, j=j: e.tensor_tensor(out=hbuf[:, j, 0:N], in0=ps[bu][:, 0:N],
                                                                         in1=sg_t[si][:, 0:N], op=ALU.mult),
                     reads=[r_ps[bu], r_sg[si]], writes=[r_h[j]])
                diag_some(2)
                if nxt is not None:
                    wt, wr = nxt
            act_pre(AF.Sqrt)
            wt, wr = ring_ds.get()
            ring_ds.held = 1
            for m in range(KC):
                nxt = None
                if m + 1 < KC:
                    nxt = ring_ds.get()
                    ring_ds.held = 2
                bd = bank_d.next()
                P.pe_group([(lambda e, j=j, wt=wt, bd=bd: e.matmul(ps[bd][:, 0:N], lhsT=wt[:, j, :], rhs=hbuf[:, j, 0:N],
                                                                   start=(j == 0), stop=(j == JC - 1)),
                             [wr, r_h[j]] + ([nxt[1]] if (nxt is not None and j == 11) else []))
                            for j in range(JC)], writes=[r_ps[bd]])
                ring_ds.held -= 1
                ring_ds.refill()
                P.op("dve", lambda e, m=m, bd=bd: e.scalar_tensor_tensor(
                    out=xT[:, m, 0:N], in0=ps[bd][:, 0:N], scalar=0.5, in1=xT[:, m, 0:N], op0=ALU.mult, op1=ALU.add),
                    reads=[r_ps[bd], r_xT[m]], writes=[r_xT[m]])
                if next_gcol is not None:
                    post_x(m, next_gcol, N)
                diag_some(10)
                if nxt is not None:
                    wt, wr = nxt
            diag_some(1000)
            if next_gcol is not None:
                stats_rstd(N)

        def mixer(t):
            col0, N = TILES[t]
            Np = min(N, PT - col0)
            Ns = N - Np
            first = (t == 0)
            last_p = (col0 + Np == PT)
            bm, bq = 6, 7
            st = {}

            def Z(c):
                wt, wr = ring_gu.get()
                ba = bank_g.next()
                bgt = bank_u.next()
                P.pe_group([(lambda e, kc=kc: e.matmul(ps[ba][:, 0:N], lhsT=wt[:, 0, kc, :], rhs=xn[:, kc, 0:N],
                                                       start=(kc == 0), stop=(kc == KC - 1)), [wr, r_xn[kc]])
                            for kc in range(KC)], writes=[r_ps[ba]])
                P.pe_group([(lambda e, kc=kc: e.matmul(ps[bgt][:, 0:N], lhsT=wt[:, 1, kc, :], rhs=xn[:, kc, 0:N],
                                                       start=(kc == 0), stop=(kc == KC - 1)), [wr, r_xn[kc]])
                            for kc in range(KC)], writes=[r_ps[bgt]])
                ring_gu.refill()
                ti = c % 2
                P.op("dve", lambda e: e.tensor_tensor(out=th_t[ti][:, 0:N], in0=ps[bgt][:, 0:N], in1=rstd[:, 0:N], op=ALU.mult),
                     reads=[r_ps[bgt], r_rstd], writes=[r_th[ti]])
                P.op("act", lambda e: e.activation(out=th_t[ti][:, 0:N], in_=th_t[ti][:, 0:N], func=AF.Tanh,
                                                   bias=hb[:, 4 + c:5 + c], scale=0.5), reads=[r_th[ti], r_hb], writes=[r_th[ti]])
                P.op("dve", lambda e: e.tensor_tensor(out=ah_t[ti][:, 0:N], in0=ps[ba][:, 0:N], in1=rstd[:, 0:N], op=ALU.mult),
                     reads=[r_ps[ba], r_rstd], writes=[r_ah[ti]])
                P.op("act", lambda e: e.activation(out=ah_t[ti][:, 0:N], in_=ah_t[ti][:, 0:N], func=AF.Identity,
                                                   bias=hb[:, c:c + 1], scale=0.5), reads=[r_ah[ti], r_hb], writes=[r_ah[ti]])
                P.op("dve", lambda e: e.scalar_tensor_tensor(
                    out=u32_t[ti][:, 0:N], in0=th_t[ti][:, 0:N], scalar=1.0, in1=ah_t[ti][:, 0:N], op0=ALU.add, op1=ALU.mult),
                    reads=[r_th[ti], r_ah[ti]], writes=[r_u32[ti]])
                P.op("act", lambda e: e.activation(out=ubf[:, c, CH:CH + Np], in_=u32_t[ti][:, 0:Np], func=AF.Copy),
                     reads=[r_u32[ti]], writes=[r_ubf[c]])
                if Ns:
                    P.op("act", lambda e: e.activation(
                        out=usbf[:, c, :, CH:CH + DSEQ], in_=u32_t[ti][:, Np:N].rearrange("p (s i) -> p s i", i=DSEQ),
                        func=AF.Copy), reads=[r_u32[ti]], writes=[r_usbf[c]])
                    P.op("dve", lambda e: e.tensor_copy(out=usnew[:, c, :], in_=u32_t[ti][:, Np:N]),
                         reads=[r_u32[ti]], writes=[r_usnew])
                if last_p:
                    P.op("dve", lambda e: e.tensor_copy(out=utail[:, c, :], in_=u32_t[ti][:, Np - CH:Np]),
                         reads=[r_u32[ti]], writes=[r_utail])

            def CONV(c):
                bc = bank_d.next()
                mm = [(lambda e, k=k: e.matmul(ps[bc][:, 0:Np], lhsT=diag[:, k * 4 + c, :], rhs=ubf[:, c, k:k + Np],
                                               start=(k == 0), stop=(k == CONV_W - 1)), [r_diag, r_ubf[c]])
                      for k in range(CONV_W)]
                if Ns:
                    mm += [(lambda e, k=k: e.matmul(
                        ps[bc][:, Np:N].rearrange("p (s i) -> p s i", i=DSEQ), lhsT=diag[:, k * 4 + c, :],
                        rhs=usbf[:, c, :, k:k + DSEQ], start=(k == 0), stop=(k == CONV_W - 1), skip_group_check=True),
                        [r_diag, r_usbf[c]]) for k in range(CONV_W)]
                P.pe_group(mm, writes=[r_ps[bc]])
                if not last_p:
                    P.op("act", lambda e: e.activation(out=ubf[:, c, 0:CH], in_=ubf[:, c, Np:Np + CH], func=AF.Copy),
                         reads=[r_ubf[c]], writes=[r_ubf[c]])
                bi = c % 2
                P.op("act", lambda e: e.activation(out=c32[:, c, 0:N], in_=ps[bc][:, 0:N], func=AF.Identity,
                                                   bias=cc(C_BDW + c)), reads=[r_ps[bc], r_consts], writes=[r_c32[c]])
                P.op("act", lambda e: e.activation(out=cbf_t[bi][:, 0:N], in_=ps[bc][:, 0:N], func=AF.Identity,
                                                   bias=cc(C_BDW + c)), reads=[r_ps[bc], r_consts], writes=[r_cbf[bi]])
                P.op("act", lambda e: e.activation(out=csq_t[bi][:, 0:N], in_=ps[bc][:, 0:N], func=AF.Square,
                                                   bias=cc(C_BDW + c)), reads=[r_ps[bc], r_consts], writes=[r_csq[bi]])

            def STAT(c):
                bi = c % 2
                P.pe_group([(lambda e: e.matmul(ps[bm][:, 0:N], lhsT=ones_ln[:, :], rhs=cbf_t[bi][:, 0:N],
                                                start=(c == 0), stop=(c == 3), skip_group_check=True), [r_cbf[bi], r_ones])],
                           writes=[r_ps[bm]] if c == 0 else [])
                st["m"] = ("pe", P.cnt["pe"])
                P.pe_group([(lambda e: e.matmul(ps[bq][:, 0:N], lhsT=ones_ln[:, :], rhs=csq_t[bi][:, 0:N],
                                                start=(c == 0), stop=(c == 3), skip_group_check=True), [r_csq[bi], r_ones])],
                           writes=[r_ps[bq]] if c == 0 else [])
                st["q"] = ("pe", P.cnt["pe"])
                if c == 3:
                    r_ps[bm].w = st["m"]
                    r_ps[bq].w = st["q"]

            def LN():
                P.op("act", lambda e: e.activation(out=ln_var[:, 0:N], in_=ps[bm][:, 0:N], func=AF.Square),
                     reads=[r_ps[bm]], writes=[r_lnv])
                P.op("dve", lambda e: e.scalar_tensor_tensor(out=ln_var[:, 0:N], in0=ps[bq][:, 0:N], scalar=EPS, in1=ln_var[:, 0:N],
                                                             op0=ALU.add, op1=ALU.subtract), reads=[r_ps[bq], r_lnv], writes=[r_lnv])
                P.op("act", lambda e: e.activation(out=ln_var[:, 0:N], in_=ln_var[:, 0:N], func=AF.Sqrt), reads=[r_lnv], writes=[r_lnv])
                act_pre(AF.Silu)

                def center(c):
                    P.op("dve", lambda e: e.tensor_tensor(out=c32[:, c, 0:N], in0=c32[:, c, 0:N], in1=ps[bm][:, 0:N], op=ALU.subtract),
                         reads=[r_c32[c], r_ps[bm]], writes=[r_c32[c]])

                def scale(c):
                    P.op("dve", lambda e: e.tensor_tensor(out=c32[:, c, 0:N], in0=c32[:, c, 0:N], in1=ln_rstd[:, 0:N], op=ALU.mult),
                         reads=[r_c32[c], r_lnr], writes=[r_c32[c]])
                    P.op("act", lambda e: e.activation(out=cq[:, c, 0:N], in_=c32[:, c, 0:N], func=AF.Silu,
                                                       bias=cc(C_LNB + c), scale=cc(C_LNG + c)),
                         reads=[r_c32[c], r_consts], writes=[r_cq[c]])

                center(0)
                P.op("dve", lambda e: e.reciprocal(out=ln_rstd[:, 0:N], in_=ln_var[:, 0:N]), reads=[r_lnv], writes=[r_lnr])
                scale(0)
                for c in range(1, 4):
                    center(c)
                    scale(c)

            def PZ(pair):
                wt, wr = ring_gu.get()
                for a in range(2):
                    g = pair * 2 + a
                    w = WINS[g]
                    if pair == 0:
                        bp = 6 if a == 0 else 7
                    else:
                        bp = bank_g.next() if a == 0 else bank_u.next()
                    P.pe_group([(lambda e, kc=kc, a=a, bp=bp: e.matmul(ps[bp][:, 0:N], lhsT=wt[:, a, kc, :], rhs=xn[:, kc, 0:N],
                                                                       start=(kc == 0), stop=(kc == KC - 1)), [wr, r_xn[kc]])
                                for kc in range(KC)], writes=[r_ps[bp]])
                    if a == 1:
                        ring_gu.refill()
                    P.op("dve", lambda e, g=g, bp=bp: e.tensor_tensor(out=pext[:, g, PH:PH + Np], in0=ps[bp][:, 0:Np], in1=rstd[:, 0:Np],
                                                                        op=ALU.mult), reads=[r_ps[bp], r_rstd], writes=[r_pext[g]])
                    P.op("act", lambda e, g=g: e.activation(out=pext[:, g, PH:PH + Np], in_=pext[:, g, PH:PH + Np], func=AF.Identity,
                                                            bias=cc(C_BIN + 8 + g)), reads=[r_pext[g], r_consts], writes=[r_pext[g]])
                    if Ns:
                        P.op("dve", lambda e, g=g, bp=bp: e.tensor_tensor(
                            out=psext[:, g, :, PH:PH + DSEQ], in0=ps[bp][:, Np:N].rearrange("p (s i) -> p s i", i=DSEQ),
                            in1=rstd[:, Np:N].rearrange("p (s i) -> p s i", i=DSEQ), op=ALU.mult),
                            reads=[r_ps[bp], r_rstd], writes=[r_psext[g]])
                        P.op("act", lambda e, g=g: e.activation(
                            out=psext[:, g, :, PH:PH + DSEQ], in_=psext[:, g, :, PH:PH + DSEQ],
                            func=AF.Identity, bias=cc(C_BIN + 8 + g)), reads=[r_psext[g], r_consts], writes=[r_psext[g]])
                    di = g
                    L = PH + Np
                    src, rsrc = pext[:, g, 0:L], r_pext[g]
                    bufs = [(sA, r_sA), (sB, r_sB)]
                    step = 1
                    bi = 0
                    while step < w:
                        dst, rdst = bufs[bi]
                        P.op("pool", lambda e, src=src, dst=dst, step=step, L=L: e.tensor_tensor(
                            out=dst[:, step:L], in0=src[:, step:L], in1=src[:, 0:L - step], op=ALU.add),
                            reads=[rsrc], writes=[rdst])
                        src, rsrc = dst[:, 0:L], rdst
                        step *= 2
                        bi ^= 1
                    P.op("dve", lambda e, src=src, g=g, di=di, w=w: e.scalar_tensor_tensor(
                        out=dd_t[di][:, 0:Np], in0=src[:, PH:PH + Np], scalar=1.0 / w, in1=pext[:, g, PH:PH + Np],
                        op0=ALU.mult, op1=ALU.subtract), reads=[rsrc, r_pext[g]], writes=[r_dd[di]])
                    if first:
                        P.op("dve", lambda e, src=src, g=g: e.tensor_tensor(
                            out=sst[:, 0:16], in0=src[:, PH:PH + 16], in1=invc[:, g * 16:(g + 1) * 16], op=ALU.mult),
                            reads=[rsrc, r_invc], writes=[r_sst])
                        P.op("dve", lambda e, g=g, di=di: e.tensor_tensor(
                            out=dd_t[di][:, 0:16], in0=sst[:, 0:16], in1=pext[:, g, PH:PH + 16], op=ALU.subtract),
                            reads=[r_sst, r_pext[g]], writes=[r_dd[di]])
                    if Ns:
                        Ls = PH + DSEQ
                        src2, rsrc2 = psext[:, g, :, :], r_psext[g]
                        bufs2 = [(sAs, r_sAs), (sBs, r_sBs)]
                        step = 1
                        bi = 0
                        while step < w:
                            dst2, rdst2 = bufs2[bi]
                            P.op("pool", lambda e, src2=src2, dst2=dst2, step=step, Ls=Ls: e.tensor_tensor(
                                out=dst2[:, :, step:Ls], in0=src2[:, :, step:Ls], in1=src2[:, :, 0:Ls - step], op=ALU.add),
                                reads=[rsrc2], writes=[rdst2])
                            src2, rsrc2 = dst2[:, :, :], rdst2
                            step *= 2
                            bi ^= 1
                        P.op("dve", lambda e, src2=src2, g=g, di=di, w=w: e.scalar_tensor_tensor(
                            out=dd_t[di][:, Np:N].rearrange("p (s i) -> p s i", i=DSEQ), in0=src2[:, :, PH:PH + DSEQ],
                            scalar=1.0 / w, in1=psext[:, g, :, PH:PH + DSEQ], op0=ALU.mult, op1=ALU.subtract),
                            reads=[rsrc2, r_psext[g]], writes=[r_dd[di]])
                        P.op("dve", lambda e, g=g: e.tensor_copy(
                            out=psnew[:, g, :].rearrange("p (s i) -> p s i", i=DSEQ), in_=psext[:, g, :, PH:PH + DSEQ]),
                            reads=[r_psext[g]], writes=[r_psnew])
                    if last_p:
                        P.op("dve", lambda e, g=g: e.tensor_copy(out=ptail[:, g, :], in_=pext[:, g, Np:Np + PH]),
                             reads=[r_pext[g]], writes=[r_ptail])
                    else:
                        P.op("pool", lambda e, g=g: e.tensor_copy(out=pext[:, g, 0:PH], in_=pext[:, g, Np:Np + PH]),
                             reads=[r_pext[g]], writes=[r_pext[g]])

            def Q(g):
                bqq = bank_d.next()
                P.pe_group([(lambda e: e.matmul(ps[bqq][:, 0:N], lhsT=wpool_b[:, g, :], rhs=dd_t[g][:, 0:N],
                                                start=True, stop=True), [r_wpb, r_dd[g]])], writes=[r_ps[bqq]])
                P.op("act", lambda e: e.activation(out=cq[:, 4 + g, 0:N], in_=ps[bqq][:, 0:N], func=AF.Identity,
                                                   scale=cc(C_PSC + g)), reads=[r_ps[bqq], r_consts], writes=[r_cq[4 + g]])

            def OUT():
                for pair in range(4):
                    wt, wr = ring_gu.get()
                    for a in range(2):
                        m = pair * 2 + a
                        bo = bank_d.next()
                        P.pe_group([(lambda e, k=k, a=a, bo=bo, wt=wt: e.matmul(ps[bo][:, 0:N], lhsT=wt[:, a, k, :], rhs=cq[:, k, 0:N],
                                                                         start=(k == 0), stop=(k == 7)), [wr, r_cq[k]])
                                    for k in range(8)], writes=[r_ps[bo]])
                        if a == 1:
                            ring_gu.refill()
                        P.op("dve", lambda e, m=m, bo=bo: e.scalar_tensor_tensor(
                            out=xT[:, m, 0:N], in0=ps[bo][:, 0:N], scalar=cc(C_BOUT + m), in1=xT[:, m, 0:N], op0=ALU.add, op1=ALU.add),
                            reads=[r_ps[bo], r_xT[m], r_consts], writes=[r_xT[m]])
                        post_x(m, C_G2, N)
                stats_rstd(N)

            Z(0)
            Z(1)
            PZ(0)
            CONV(0)
            Z(2)
            STAT(0)
            PZ(1)
            CONV(1)
            Z(3)
            act_pre(AF.Sqrt)
            STAT(1)
            CONV(2)
            for g in range(4):
                Q(g)
            STAT(2)
            CONV(3)
            STAT(3)
            LN()
            act_pre(AF.Sqrt)
            fb = bank_g.next()
            P.pe_group([(lambda e: e.matmul(ps[fb][:, 0:N], lhsT=ones_ln[:, :], rhs=xn[:, 0, 0:N], start=True, stop=True),
                         [r_ones, r_xn[0]]) for _ in range(24)], writes=[r_ps[fb]])
            OUT()

        out_toks = []

        bank_f = RR([7, 6, 0, 2, 1, 3])

        def final(t):
            col0, N = TILES[t]
            nblk = (N + 127) // 128
            for b in range(nblk):
                rows = min(128, N - b * 128)
                so = rr_stgo.next()
                bks = []
                for half in range(2):
                    bk = bank_f.next()
                    bks.append(bk)
                    P.pe_group([(lambda e, cl=cl, bk=bk, rows=rows, half=half, b=b: e.transpose(
                        out=ps[bk][0:rows, cl * 128:(cl + 1) * 128], in_=xT[:, half * 4 + cl, b * 128:b * 128 + rows],
                        identity=ident32[:, :]), [r_xT[half * 4 + cl], r_ident32]) for cl in range(4)], writes=[r_ps[bk]])
                    P.op("act", lambda e, bk=bk, rows=rows, half=half, so=so: e.activation(
                        out=stg_st[0:rows, :], in_=ps[bk][0:rows, :], func=AF.Square, accum_out=rcol[so][0:rows, half:half + 1]),
                        reads=[r_ps[bk]], writes=[r_stgst, r_rcol[so]])
                P.op("dve", lambda e, rows=rows, so=so: e.tensor_tensor(out=rcol[so][0:rows, 2:3], in0=rcol[so][0:rows, 0:1],
                                                                        in1=rcol[so][0:rows, 1:2], op=ALU.add),
                     reads=[r_rcol[so]], writes=[r_rcol[so]])
                P.op("act", lambda e, rows=rows, so=so: e.activation(out=rcol[so][0:rows, 3:4], in_=rcol[so][0:rows, 2:3], func=AF.Sqrt,
                                                                     bias=epsc[0:rows, 0:1], scale=1.0 / D),
                     reads=[r_rcol[so], r_mhalf], writes=[r_rcol[so]])
                P.op("dve", lambda e, rows=rows, so=so: e.reciprocal(out=rcol[so][0:rows, 4:5], in_=rcol[so][0:rows, 3:4]),
                     reads=[r_rcol[so]], writes=[r_rcol[so]])
                for half in range(2):
                    bk = bks[half]
                    P.op("dve", lambda e, bk=bk, rows=rows, half=half, so=so: e.scalar_tensor_tensor(
                        out=stg_out[so][0:rows, half * 512:(half + 1) * 512], in0=ps[bk][0:rows, :], scalar=rcol[so][0:rows, 4:5],
                        in1=gf_bc[0:rows, half * 512:(half + 1) * 512], op0=ALU.mult, op1=ALU.mult),
                        reads=[r_ps[bk], r_rcol[so], r_consts], writes=[r_stgo[so]])
                fns = []
                for kind, r0, poff, nr in segments(col0 + b * 128, rows):
                    if kind == "meta":
                        continue
                    fns.append(lambda e, sm, kind=kind, r0=r0, poff=poff, nr=nr, so=so: e.dma_start(
                        out=out_dst[kind][r0:r0 + nr, :], in_=stg_out[so][poff:poff + nr, :]).then_inc(sm, 16))
                if fns:
                    P.dma("sp", fns, sem_stgo[so], reads=[r_stgo[so]])

        def state_outputs():
            for (src, rs, rows, dst) in ((utail, r_utail, CH, ncp), (ptail, r_ptail, PH, npp)):
                bk = bank_t.next()
                P.pe_group([(lambda e, c=c, bk=bk, src=src, rows=rows: e.transpose(
                    out=ps[bk][0:rows, c * 128:(c + 1) * 128], in_=src[:, c, :], identity=ident32[:, :]), [rs, r_ident32])
                    for c in range(4)], writes=[r_ps[bk]])
                so = rr_so.next()
                P.op("act", lambda e, bk=bk, rows=rows, so=so: e.activation(out=stg_so[so][0:rows, :], in_=ps[bk][0:rows, :], func=AF.Copy),
                     reads=[r_ps[bk]], writes=[r_stgso[so]])
                P.dma("sp", [lambda e, sm, so=so, rows=rows, dst=dst: e.dma_start(out=dst[:, :], in_=stg_so[so][0:rows, :]).then_inc(sm, 16)],
                      sem_so[so], reads=[r_stgso[so]])
            for (src, rs, hist, dst) in ((usnew, r_usnew, CH, ncs), (psnew, r_psnew, PH, nps)):
                bk = bank_t.next()
                P.pe_group([(lambda e, c=c, bk=bk, src=src: e.transpose(
                    out=ps[bk][0:ST, c * 128:(c + 1) * 128], in_=src[:, c, :], identity=ident32[:, :]), [rs, r_ident32])
                    for c in range(4)], writes=[r_ps[bk]])
                so = rr_so.next()
                P.op("act", lambda e, bk=bk, so=so: e.activation(out=stg_so[so][0:ST, :], in_=ps[bk][0:ST, :], func=AF.Copy),
                     reads=[r_ps[bk]], writes=[r_stgso[so]])
                fns = []
                for s in range(NSS):
                    r0 = s * hist + hist - DSEQ
                    fns.append(lambda e, sm, s=s, r0=r0, so=so, dst=dst: e.dma_start(
                        out=dst[r0:r0 + DSEQ, :], in_=stg_so[so][s * DSEQ:(s + 1) * DSEQ, :]).then_inc(sm, 16))
                P.dma("sp", fns, sem_so[so], reads=[r_stgso[so]])

        blocks = load_x_dma(0)
        q0, f0, rd0, p0 = gu_loads[0]
        gu_loads[0] = (q0, f0, list(rd0) + [r_consts, r_stgi[0]], p0)
        ring_gu.refill()
        ring_ds.refill()
        for t in range(len(TILES)):
            col0, N = TILES[t]
            load_x_transpose(t, blocks)
            dump(t, 0)
            for c in range(KC):
                post_x(c, C_G1, N)
            stats_rstd(N)
            ffn(N, C_GM)
            if t == 0:
                load_states()
            dump(t, 1)
            mixer(t)
            if t == len(TILES) - 1:
                state_outputs()
            dump(t, 2)
            if t + 1 < len(TILES):
                blocks = load_x_dma(t + 1)
            ffn(N, None)
            dump(t, 3)
            final(t)
            dump(t, 4)
        assert ring_gu.consumed == len(gu_loads) and ring_ds.consumed == len(ds_loads)

        sem_names = list(Prog.QUEUES[:4]) + P.dma_sems
        sem_ctx = {}
        for name in sem_names:
            sem_ctx[name] = es.enter_context(nc.semaphore(name))
        final_waits = [(s, P.cnt[s]) for s in P.dma_sems if P.cnt[s] > 0 and (s.startswith("stgo") or s.startswith("stgi") or s.startswith("so") or s == "d2d" or s == "dbg")]
        block = es.enter_context(nc.Block())

        @block.tensor
        def _(e):
            P.replay("pe", e, sem_ctx)

        @block.scalar
        def _(e):
            P.replay("act", e, sem_ctx)

        @block.vector
        def _(e):
            P.replay("dve", e, sem_ctx)

        @block.gpsimd
        def _(e):
            P.replay("pool", e, sem_ctx)

        @block.sync
        def _(e):
            P.replay("sp", e, sem_ctx)
            for s, v in final_waits:
                e.wait_ge(sem_ctx[s], v)
    return nc


_CACHE = {}


def _consts_array(inp):
    c = np.zeros((128, NCONST), np.float32)

    def put(col, vec):
        v = np.asarray(vec, np.float32).reshape(-1, 128)
        c[:, col:col + v.shape[0]] = v.T

    put(C_G1, inp["norm_ffn1"][0])
    put(C_GM, inp["norm_mix"][0])
    put(C_G2, inp["norm_ffn2"][0])
    put(C_GF, inp["norm_final"])
    put(C_BIN, inp["b_in"][0])
    put(C_BDW, inp["b_dw"][0])
    put(C_LNG, inp["ln_conv_g"][0])
    put(C_LNB, inp["ln_conv_b"][0])
    put(C_PSC, inp["pool_scale"][0])
    put(C_BOUT, inp["b_out"][0])
    wdw = np.asarray(inp["w_dw"][0], np.float32)
    for k in range(CONV_W):
        put(C_WDW + k * 4, wdw[k])
    return c


def _gu_layout(wa, wb, cols_a, cols_b):
    n = len(cols_a)
    out = np.empty((n, 128, 2, KC, 128), np.float32)
    wa4 = wa.reshape(KC, 128, -1, 128)
    wb4 = wb.reshape(KC, 128, -1, 128)
    for i in range(n):
        out[i, :, 0] = wa4[:, :, cols_a[i], :].transpose(1, 0, 2)
        out[i, :, 1] = wb4[:, :, cols_b[i], :].transpose(1, 0, 2)
    return out.reshape(n, 128, 2 * KC * 128)


def _d_layout(wdn):
    w4 = wdn.reshape(JC, 128, KC, 128)
    return np.ascontiguousarray(w4.transpose(2, 1, 0, 3)).reshape(KC, 128, JC * 128)


def kernel(**inp):
    inp = {k: np.asarray(v) for k, v in inp.items()}
    if "nc" not in _CACHE:
        _CACHE["nc"] = build_program()
    nc = _CACHE["nc"]

    w1g, w1u, w1d = inp["w_ffn1_gate"][0], inp["w_ffn1_up"][0], inp["w_ffn1_down"][0]
    w2g, w2u, w2d = inp["w_ffn2_gate"][0], inp["w_ffn2_up"][0], inp["w_ffn2_down"][0]
    w_in, w_out = inp["w_in"][0], inp["w_out"][0]
    shared = {
        "wgu1": _gu_layout(w1g, w1u, list(range(JC)), list(range(JC))),
        "wgu2": _gu_layout(w2g, w2u, list(range(JC)), list(range(JC))),
        "wd1": _d_layout(w1d),
        "wd2": _d_layout(w2d),
        "win": _gu_layout(w_in, w_in, [0, 1, 2, 3, 8, 10], [4, 5, 6, 7, 9, 11]),
        "wout": _gu_layout(w_out, w_out, [0, 2, 4, 6], [1, 3, 5, 7]),
        "wpool": np.ascontiguousarray(inp["w_pool"][0].transpose(1, 0, 2)).reshape(128, 512).astype(np.float32),
        "consts": _consts_array(inp),
        "ident": np.eye(128, dtype=np.float32),
        "meta": np.ascontiguousarray(inp["meta_tokens"], np.float32),
    }
    invc = np.zeros((128, 64), np.float32)
    for g, w in enumerate(WINS):
        invc[:, g * 16:(g + 1) * 16] = 1.0 / np.minimum(np.arange(16) + 1, w).astype(np.float32)
    shared["invc"] = invc
    shared["gfbc"] = np.ascontiguousarray(np.broadcast_to(np.asarray(inp["norm_final"], np.float32)[None, :], (128, D)))

    in_maps = []
    for c in range(8):
        m = dict(shared)
        m["xp"] = np.ascontiguousarray(inp["x_prompt"][c], np.float32)
        m["xs"] = np.ascontiguousarray(inp["x_sample"][c * NSS:(c + 1) * NSS].reshape(ST, D), np.float32)
        m["sconv"] = np.ascontiguousarray(inp["state_conv"][0, c * NSS:(c + 1) * NSS].reshape(NSS * CH, CC), np.float32)
        m["spool"] = np.ascontiguousarray(inp["state_pool"][0, c * NSS:(c + 1) * NSS].reshape(NSS * PH, CC), np.float32)
        in_maps.append(m)
    res = run_bass_kernel_spmd(nc, in_maps, core_ids=list(range(8)))
    rs = res.results
    _CACHE["last"] = rs
    y_prompt = np.stack([rs[c]["yp"] for c in range(8)], 0)
    y_sample = np.concatenate([rs[c]["ys"].reshape(NSS, DSEQ, D) for c in range(8)], 0)
    ncp_o = np.stack([rs[c]["ncp"] for c in range(8)], 0)[None]
    npp_o = np.stack([rs[c]["npp"] for c in range(8)], 0)[None]
    ncs_o = np.concatenate([rs[c]["ncs"].reshape(NSS, CH, CC) for c in range(8)], 0)[None]
    nps_o = np.concatenate([rs[c]["nps"].reshape(NSS, PH, CC) for c in range(8)], 0)[None]
    return (y_prompt.astype(np.float32), y_sample.astype(np.float32), ncp_o.astype(np.float32),
            npp_o.astype(np.float32), ncs_o.astype(np.float32), nps_o.astype(np.float32))
```

```python
import numpy as np
import concourse.bass as bass
import concourse.mybir as mybir
from concourse.bass_utils import run_bass_kernel_spmd

F32 = mybir.dt.float32
BF16 = mybir.dt.bfloat16
AF = mybir.ActivationFunctionType
ALU = mybir.AluOpType

D = 1024
KC = 8
DFF = 2816
JC = 22
CC = 512
NMETA = 16
SEQ = 2048
PT = NMETA + SEQ
NSS = 16
DSEQ = 4
ST = NSS * DSEQ
T = PT + ST
CONV_W = 31
CH = 30
PH = 15
WINS = (2, 4, 8, 16)
EPS = 1e-6
TILES = [(0, 432), (432, 432), (864, 432), (1296, 432), (1728, 400)]
NMAX = 432

C_G1, C_GM, C_G2, C_GF, C_BIN, C_BDW, C_LNG, C_LNB, C_PSC, C_BOUT, C_WDW = 0, 8, 16, 24, 32, 44, 48, 52, 56, 60, 68
NCONST = C_WDW + CONV_W * 4

N_GU = 6
N_DS = 4
WEIGHT_MODE = "stream32"


class Res:
    __slots__ = ("w", "r", "name")

    def __init__(self, name=""):
        self.w = None
        self.r = []
        self.name = name


class Prog:
    QUEUES = ("pe", "act", "dve", "pool", "sp")

    def __init__(self):
        self.q = {k: [] for k in self.QUEUES}
        self.cnt = {k: 0 for k in self.QUEUES}
        self.waited = {k: {} for k in self.QUEUES}
        self.dma_sems = []

    def new_dma_sem(self, name):
        self.cnt[name] = 0
        self.dma_sems.append(name)
        return name

    def _waits(self, q, deps):
        out = []
        for s, v in deps.items():
            if q == "pe" and s == "pe":
                continue
            if self.waited[q].get(s, 0) >= v:
                continue
            self.waited[q][s] = v
            out.append((s, v))
        return out

    @staticmethod
    def _flat(rs):
        out = []
        for r in rs:
            if isinstance(r, (list, tuple)):
                out.extend(Prog._flat(r))
            else:
                out.append(r)
        return out

    @staticmethod
    def _add(deps, tok):
        if tok is None:
            return
        s, v = tok
        if deps.get(s, 0) < v:
            deps[s] = v

    def op(self, q, fn, reads=(), writes=()):
        reads = self._flat(reads)
        writes = self._flat(writes)
        deps = {}
        for r in reads:
            self._add(deps, r.w)
        for r in writes:
            self._add(deps, r.w)
            for t in r.r:
                self._add(deps, t)
        waits = self._waits(q, deps)
        self.cnt[q] += 1
        tok = (q, self.cnt[q])
        self.q[q].append((waits, fn, True))
        for r in reads:
            r.r.append(tok)
        for r in writes:
            r.w = tok
            r.r = []
        return tok

    def pe_group(self, mms, writes):
        n = len(mms)
        allreads = []
        writes = self._flat(writes)
        for i, (fn, reads) in enumerate(mms):
            reads = self._flat(reads)
            deps = {}
            for r in reads:
                self._add(deps, r.w)
            if i == 0:
                for r in writes:
                    self._add(deps, r.w)
                    for t in r.r:
                        self._add(deps, t)
            waits = self._waits("pe", deps)
            self.q["pe"].append((waits, fn, i == n - 1))
            allreads.extend(reads)
        self.cnt["pe"] += 1
        tok = ("pe", self.cnt["pe"])
        seen = set()
        for r in allreads:
            if id(r) in seen:
                continue
            seen.add(id(r))
            r.r.append(tok)
        for r in writes:
            r.w = tok
            r.r = []
        return tok

    def dma(self, q, fns, sem, reads=(), writes=()):
        reads = self._flat(reads)
        writes = self._flat(writes)
        deps = {}
        for r in reads:
            self._add(deps, r.w)
        for r in writes:
            self._add(deps, r.w)
            for t in r.r:
                self._add(deps, t)
        waits = self._waits(q, deps)
        for i, fn in enumerate(fns):
            self.q[q].append((waits if i == 0 else [], (fn, sem), None))
            self.cnt[sem] += 16
        tok = (sem, self.cnt[sem])
        for r in reads:
            r.r.append(tok)
        for r in writes:
            r.w = tok
            r.r = []
        return tok

    def replay(self, q, eng, sems):
        for waits, fn, inc in self.q[q]:
            for s, v in waits:
                eng.wait_ge(sems[s], v)
            if inc is None:
                f, sem = fn
                f(eng, sems[sem])
            else:
                ins = fn(eng)
                if inc:
                    ins.then_inc(sems[q], 1)


class Ring:
    def __init__(self, P, name, nslots, tensors, loads):
        self.P = P
        self.n = nslots
        self.t = tensors
        self.res = [Res(f"{name}{i}") for i in range(nslots)]
        self.sems = {"sp": [P.new_dma_sem(f"{name}{i}") for i in range(nslots)],
                     "pool": [P.new_dma_sem(f"{name}c{i}") for i in range(nslots)]}
        self.loads = loads
        self.emitted = 0
        self.consumed = 0

    def refill(self):
        lim = min(len(self.loads), self.consumed + self.n)
        while self.emitted < lim:
            i = self.emitted
            s = i % self.n
            queue, ld, rd, post = self.loads[i]
            dst = self.t[s]
            self.P.dma(queue, [lambda e, sm, ld=ld, dst=dst: ld(e, sm, dst)], self.sems[queue][s], reads=rd, writes=[self.res[s]])
            if post is not None:
                post(dst, self.res[s])
            self.emitted += 1

    def get(self):
        self.refill()
        i = self.consumed
        assert i < self.emitted, "ring underflow"
        self.consumed += 1
        s = i % self.n
        return self.t[s], self.res[s]


def build_program():
    nc = bass.Bass("TRN2", target_bir_lowering=False)
    P = Prog()

    def din(name, shape):
        return nc.dram_tensor(name, list(shape), F32, kind="ExternalInput").ap()

    def dout(name, shape):
        return nc.dram_tensor(name, list(shape), F32, kind="ExternalOutput").ap()

    xp = din("xp", (SEQ, D))
    xs = din("xs", (ST, D))
    meta = din("meta", (NMETA, D))
    sconv = din("sconv", (NSS * CH, CC))
    spool = din("spool", (NSS * PH, CC))
    wgu = [din("wgu1", (JC, 128, 2 * KC * 128)), din("wgu2", (JC, 128, 2 * KC * 128))]
    wd = [din("wd1", (KC, 128, JC * 128)), din("wd2", (KC, 128, JC * 128))]
    win = din("win", (6, 128, 2 * KC * 128))
    wout = din("wout", (4, 128, 2 * KC * 128))
    wpool = din("wpool", (128, 4 * 128))
    consts_d = din("consts", (128, NCONST))
    ident_d = din("ident", (128, 128))
    invc_d = din("invc", (128, 4 * 16))
    gfbc_d = din("gfbc", (128, D))

    yp = dout("yp", (SEQ, D))
    ys = dout("ys", (ST, D))
    ncp = dout("ncp", (CH, CC))
    npp = dout("npp", (PH, CC))
    ncs = dout("ncs", (NSS * CH, CC))
    nps = dout("nps", (NSS * PH, CC))

    import os
    DBG = int(os.environ.get("KDBG", "-1"))
    dbg_d = [dout(f"dbg{i}", (128, KC * NMAX)) for i in range(5)] if DBG >= 0 else None
    from contextlib import ExitStack
    es = ExitStack()

    def sb(name, shape, dt=F32):
        return es.enter_context(nc.sbuf_tensor("sb_" + name, list(shape), dt))

    def pst(name):
        return es.enter_context(nc.psum_tensor(name, [128, 512], F32))

    with es:
        consts = sb("consts", (128, NCONST))
        hb = sb("hb", (128, 8))
        ident32 = sb("ident32", (128, 128))
        ones_rms = sb("ones_rms", (128, 128), BF16)
        ones_ln = sb("ones_ln", (128, 128), BF16)
        invc = sb("invc", (128, 4 * 16))
        gf_bc = sb("gf_bc", (128, D))
        ones32 = sb("ones32", (128, 8))
        rcol = [sb(f"rcol{i}", (128, 8)) for i in range(3)]
        mhalf = sb("mhalf", (128, 8))
        dummy = sb("dummy", (128, 8))
        epsc = mhalf
        diag = sb("diag", (128, CONV_W * 4, 128), BF16)
        wpool_b = sb("wpool_b", (128, 4, 128), BF16)

        xT = sb("xT", (128, KC, NMAX))
        xn = sb("xn", (128, KC, NMAX), BF16)
        hbuf = sb("hbuf", (128, JC, NMAX), BF16)
        gu_t = [sb(f"gu{i}", (128, 2, KC, 128), BF16) for i in range(N_GU)]
        ds_t = [sb(f"ds{i}", (128, JC, 128), BF16) for i in range(N_DS)]
        sg_t = [sb(f"sg{i}", (128, NMAX)) for i in range(3)]
        sq_t = [sb(f"sq{i}", (128, NMAX), BF16) for i in range(8)]
        sst = sb("sst", (128, NMAX))
        rstd = sb("rstd", (128, NMAX))
        stg_in = [sb(f"stgi{i}", (128, D)) for i in range(2)]
        stg_out = [sb(f"stgo{i}", (128, D)) for i in range(3)]
        th_t = [sb(f"th{i}", (128, NMAX)) for i in range(2)]
        ah_t = [sb(f"ah{i}", (128, NMAX)) for i in range(2)]
        u32_t = [sb(f"u32{i}", (128, NMAX)) for i in range(2)]
        ubf = sb("ubf", (128, 4, CH + NMAX), BF16)
        usbf = sb("usbf", (128, 4, NSS, CH + DSEQ), BF16)
        utail = sb("utail", (128, 4, CH))
        usnew = sb("usnew", (128, 4, ST))
        c32 = hbuf[:, 0:8, :].rearrange("p j n -> p (j n)").bitcast(F32).rearrange("p (c n) -> p c n", c=4)
        cbf_t = [sb(f"cbf{i}", (128, NMAX), BF16) for i in range(2)]
        csq_t = [sb(f"csq{i}", (128, NMAX), BF16) for i in range(2)]
        ln_mean = hbuf[:, 16:18, :].rearrange("p j n -> p (j n)").bitcast(F32)
        ln_var = hbuf[:, 18:20, :].rearrange("p j n -> p (j n)").bitcast(F32)
        ln_rstd = sb("ln_rstd", (128, NMAX))
        cq = hbuf[:, 8:16, :]
        pext = sb("pext", (128, 4, PH + NMAX))
        psext = sb("psext", (128, 4, NSS, PH + DSEQ))
        sA = sb("sA", (128, PH + NMAX))
        sB = sb("sB", (128, PH + NMAX))
        sAs = sA[:, 0:NSS * (PH + DSEQ)].rearrange("p (s r) -> p s r", r=PH + DSEQ)
        sBs = sB[:, 0:NSS * (PH + DSEQ)].rearrange("p (s r) -> p s r", r=PH + DSEQ)
        dd_t = [sb(f"dd{i}", (128, NMAX), BF16) for i in range(4)]
        ptail = sb("ptail", (128, 4, PH))
        psnew = sb("psnew", (128, 4, ST))
        stg_st = hbuf[:, 0:3, :].rearrange("p j n -> p (j n)").bitcast(F32)[:, 0:CC]
        wpool_f = stg_st
        stg_so = [stg_out[i][:, 0:CC] for i in range(2)]

        ps = [pst(f"ps{i}") for i in range(8)]

        R = Res
        r_consts, r_hb, r_ident32, r_identb, r_ones, r_invc, r_diag, r_wpf, r_wpb = (R() for _ in range(9))
        r_mhalf = R()
        r_dummy = R()
        r_rcol = [R(), R(), R()]
        r_xT = [R(f"xT{c}") for c in range(KC)]
        r_xn = [R(f"xn{c}") for c in range(KC)]
        r_h = [R(f"h{j}") for j in range(JC)]
        r_sg = [R() for _ in range(3)]
        r_sq = [R() for _ in range(8)]
        r_sst, r_rstd, r_rscr = R(), R(), R()
        r_stgi = [R(), R()]
        r_stgo = [R(), R(), R()]
        r_ps = [R(f"ps{i}") for i in range(8)]
        r_th = [R(), R()]
        r_ah = [R(), R()]
        r_u32 = [R(), R()]
        r_ubf = [R() for _ in range(4)]
        r_usbf = [R() for _ in range(4)]
        r_utail, r_usnew = R(), R()
        r_c32 = [[r_h[2 * c], r_h[2 * c + 1]] for c in range(4)]
        r_cbf = [R(), R()]
        r_csq = [R(), R()]
        r_lnm, r_lnv, r_lnr, r_lnmr = [r_h[16], r_h[17]], [r_h[18], r_h[19]], R(), R()
        r_cq = [r_h[8 + k] for k in range(8)]
        r_pext = [R() for _ in range(4)]
        r_psext = [R() for _ in range(4)]
        r_sA, r_sB = R(), R()
        r_sAs, r_sBs = r_sA, r_sB
        r_dd = [R() for _ in range(4)]
        r_ptail, r_psnew = R(), R()
        r_stgst = [r_h[0], r_h[1], r_h[2]]
        r_wpf = r_stgst
        r_stgso = r_stgo

        sem_misc = P.new_dma_sem("misc")
        sem_stgi = [P.new_dma_sem("stgi0"), P.new_dma_sem("stgi1")]
        sem_stgo = [P.new_dma_sem("stgo0"), P.new_dma_sem("stgo1"), P.new_dma_sem("stgo2")]
        sem_st = P.new_dma_sem("st")
        sem_so = [P.new_dma_sem(f"so{i}") for i in range(2)]
        sem_d2d = P.new_dma_sem("d2d")
        sem_dbg = P.new_dma_sem("dbg")

        def dump(t, i):
            if DBG == t:
                P.dma("sp", [lambda e, sm, i=i: e.dma_start(out=dbg_d[i][:, :], in_=xT[:, :, :].rearrange("p c n -> p (c n)")).then_inc(sm, 16)],
                      sem_dbg, reads=r_xT)

        class RR:
            def __init__(self, items):
                self.items = items
                self.i = 0

            def next(self):
                v = self.items[self.i % len(self.items)]
                self.i += 1
                return v

        bank_g = RR([0, 1])
        bank_u = RR([2, 3])
        bank_d = RR([4, 5])
        bank_s = RR([6])
        bank_t = RR([7, 6])
        rr_sg = RR([0, 1, 2])
        rr_sq = RR(list(range(8)))
        rr_stgi = RR([0, 1])
        rr_stgo = RR([0, 1, 2])
        rr2 = {k: RR([0, 1]) for k in ("th", "ah", "u32", "cbf", "csq", "dd")}
        rr_so = RR([0, 1])

        wgu_b = [nc.dram_tensor("wgu1_b", [JC, 128, 2 * KC * 128], BF16).ap(), nc.dram_tensor("wgu2_b", [JC, 128, 2 * KC * 128], BF16).ap()]
        wd_b = [nc.dram_tensor("wd1_b", [KC, 128, JC * 128], BF16).ap(), nc.dram_tensor("wd2_b", [KC, 128, JC * 128], BF16).ap()]
        win_b = nc.dram_tensor("win_b", [6, 128, 2 * KC * 128], BF16).ap()
        wout_b = nc.dram_tensor("wout_b", [4, 128, 2 * KC * 128], BF16).ap()
        NPC = 8
        pc_sems = [P.new_dma_sem(f"pc{i}") for i in range(NPC)]
        pc_slot_res = [Res() for _ in range(NPC)]
        pc_state = {"n": 0}
        chunk_res = {}

        def mk_loads(src32, srcb, i, kind, maxlast):
            key = (id(srcb), i)
            chunk_res[key] = Res()
            flat = (lambda dst: dst[:, :, :, :].rearrange("p a k n -> p (a k n)")) if kind == "gu" else \
                   (lambda dst: dst[:, :, :].rearrange("p j n -> p (j n)"))

            def ld0(e, sm, dst):
                e.dma_start(out=flat(dst), in_=src32[i, :, :], max_dma_last_dim=maxlast).then_inc(sm, 16)

            def post0(dst, slot_res):
                sl = pc_state["n"] % NPC
                pc_state["n"] += 1
                P.dma("sp", [lambda e, sm: e.dma_start(out=srcb[i, :, :], in_=flat(dst)).then_inc(sm, 16)], pc_sems[sl],
                      reads=[slot_res], writes=[chunk_res[key], pc_slot_res[sl]])

            def ld1(e, sm, dst):
                e.dma_start(out=flat(dst), in_=srcb[i, :, :]).then_inc(sm, 16)

            return ("pool", ld0, [], post0), ("sp", ld1, [chunk_res[key]], None), ("pool", ld0, [], None)

        gu_seq = ([(wgu[0], wgu_b[0], j) for j in range(JC)] + [(win, win_b, i) for i in (0, 1, 4, 2, 5, 3)] +
                  [(wout, wout_b, i) for i in range(4)] + [(wgu[1], wgu_b[1], j) for j in range(JC)])
        ds_seq = [(wd[0], wd_b[0], m) for m in range(KC)] + [(wd[1], wd_b[1], m) for m in range(KC)]
        gu_pairs = [mk_loads(a_, b_, i, "gu", 8192) for (a_, b_, i) in gu_seq]
        ds_pairs = [mk_loads(a_, b_, i, "ds", 5632) for (a_, b_, i) in ds_seq]
        def seq_loads(pairs):
            out = []
            out += [p[0] if i % 2 == 0 else p[2] for i, p in enumerate(pairs)]
            out += [p[1] if i % 2 == 0 else p[0] for i, p in enumerate(pairs)]
            for _t in range(2, len(TILES)):
                out += [p[1] for p in pairs]
            return out
        gu_loads = seq_loads(gu_pairs)
        ds_loads = seq_loads(ds_pairs)
        if WEIGHT_MODE == "stream32":
            gu_loads = [p[2] for p in gu_pairs] * len(TILES)
            ds_loads = [p[2] for p in ds_pairs] * len(TILES)
        ring_gu = Ring(P, "rgu", N_GU, gu_t, gu_loads)
        ring_ds = Ring(P, "rds", N_DS, ds_t, ds_loads)

        def cc(col, n=1):
            return consts[:, col:col + n]

        P.dma("sp", [lambda e, sm: e.dma_start(out=consts[:, :], in_=consts_d[:, :]).then_inc(sm, 16),
                     lambda e, sm: e.dma_start(out=ident32[:, :], in_=ident_d[:, :]).then_inc(sm, 16),
                     lambda e, sm: e.dma_start(out=invc[:, :], in_=invc_d[:, :]).then_inc(sm, 16),
                     lambda e, sm: e.dma_start(out=wpool_f[:, :], in_=wpool[:, :]).then_inc(sm, 16),
                     lambda e, sm: e.dma_start(out=gf_bc[:, :], in_=gfbc_d[:, :]).then_inc(sm, 16)],
              sem_misc, writes=[r_consts, r_ident32, r_invc, r_wpf])

        P.op("dve", lambda e: e.memset(mhalf[:, :], EPS), writes=[r_mhalf])
        P.op("dve", lambda e: e.memset(ones32[:, :], 1.0), writes=[r_ones])
        P.op("dve", lambda e: e.memset(ones_rms[:, :], 1.0 / D), writes=[r_ones])
        P.op("dve", lambda e: e.memset(ones_ln[:, :], 1.0 / CC), writes=[r_ones])
        P.op("dve", lambda e: e.tensor_scalar(out=hb[:, :], in0=consts[:, C_BIN:C_BIN + 8], scalar1=0.5, scalar2=None,
                                              op0=ALU.mult), reads=[r_consts], writes=[r_hb])
        P.op("dve", lambda e: e.tensor_copy(out=wpool_b[:, :, :].rearrange("p g n -> p (g n)"), in_=wpool_f[:, :]),
             reads=[r_wpf], writes=[r_wpb])
        P.op("pool", lambda e: e.memset(sA[:, :], 0.0), writes=[r_sA])
        P.op("pool", lambda e: e.memset(sB[:, :], 0.0), writes=[r_sB])
        P.op("pool", lambda e: e.memset(sAs[:, :, :], 0.0), writes=[r_sAs])
        P.op("pool", lambda e: e.memset(sBs[:, :, :], 0.0), writes=[r_sBs])
        P.op("dve", lambda e: e.memset(ubf[:, :, 0:CH], 0.0), writes=r_ubf)
        P.op("dve", lambda e: e.memset(pext[:, :, 0:PH], 0.0), writes=r_pext)
        diag_todo = [(k, c) for k in range(CONV_W) for c in range(4)]

        def diag_some(n):
            for _ in range(min(n, len(diag_todo))):
                k, c = diag_todo.pop(0)
                P.op("dve", lambda e, k=k, c=c: e.tensor_scalar(
                    out=diag[:, k * 4 + c, :], in0=ident32[:, :], scalar1=cc(C_WDW + k * 4 + c), scalar2=None,
                    op0=ALU.mult), reads=[r_ident32, r_consts], writes=[r_diag])

        def load_states():
          for blk in range(4):
              P.dma("sp", [lambda e, sm, blk=blk: e.dma_start(out=stg_st[0:120, :], in_=sconv[blk * 120:(blk + 1) * 120, :]
                                                               ).then_inc(sm, 16)], sem_st, writes=[r_stgst])
              b = bank_t.next()
              P.pe_group([(lambda e, c=c, b=b: e.transpose(out=ps[b][:, c * 128:c * 128 + 120],
                                                           in_=stg_st[0:120, c * 128:(c + 1) * 128],
                                                           identity=ident32[0:120, 0:120]), [r_stgst, r_ident32])
                          for c in range(4)], writes=[r_ps[b]])
              for c in range(4):
                  P.op("act", lambda e, c=c, b=b, blk=blk: e.activation(
                      out=usbf[:, c, blk * 4:(blk + 1) * 4, 0:CH],
                      in_=ps[b][:, c * 128:c * 128 + 120].rearrange("p (s r) -> p s r", r=CH), func=AF.Copy),
                      reads=[r_ps[b]], writes=[r_usbf[c]])
          for blk in range(2):
              P.dma("sp", [lambda e, sm, blk=blk: e.dma_start(out=stg_st[0:120, :], in_=spool[blk * 120:(blk + 1) * 120, :]
                                                               ).then_inc(sm, 16)], sem_st, writes=[r_stgst])
              b = bank_t.next()
              P.pe_group([(lambda e, c=c, b=b: e.transpose(out=ps[b][:, c * 128:c * 128 + 120],
                                                           in_=stg_st[0:120, c * 128:(c + 1) * 128],
                                                           identity=ident32[0:120, 0:120]), [r_stgst, r_ident32])
                          for c in range(4)], writes=[r_ps[b]])
              for c in range(4):
                  P.op("act", lambda e, c=c, b=b, blk=blk: e.activation(
                      out=psext[:, c, blk * 8:(blk + 1) * 8, 0:PH],
                      in_=ps[b][:, c * 128:c * 128 + 120].rearrange("p (s r) -> p s r", r=PH), func=AF.Copy),
                      reads=[r_ps[b]], writes=[r_psext[c]])

        P.dma("sp", [lambda e, sm: e.dma_start(
            out=ncs.rearrange("(s r) c -> s (r c)", r=CH)[:, 0:(CH - DSEQ) * CC],
            in_=sconv.rearrange("(s r) c -> s (r c)", r=CH)[:, DSEQ * CC:CH * CC]).then_inc(sm, 16),
            lambda e, sm: e.dma_start(
            out=nps.rearrange("(s r) c -> s (r c)", r=PH)[:, 0:(PH - DSEQ) * CC],
            in_=spool.rearrange("(s r) c -> s (r c)", r=PH)[:, DSEQ * CC:PH * CC]).then_inc(sm, 16)], sem_d2d)

        def segments(col, n):
            out = []
            c = col
            end = col + n
            while c < end:
                if c < NMETA:
                    e = min(end, NMETA)
                    out.append(("meta", c, c - col, e - c))
                elif c < PT:
                    e = min(end, PT)
                    out.append(("p", c - NMETA, c - col, e - c))
                else:
                    e = end
                    out.append(("s", c - PT, c - col, e - c))
                c = e
            return out

        in_src = {"meta": meta, "p": xp, "s": xs}
        out_dst = {"p": yp, "s": ys}

        stg_all = stg_in + stg_out
        r_stg_all = r_stgi + r_stgo
        sem_stg_all = sem_stgi + sem_stgo

        def load_x_dma_block(t, b):
            col0, N = TILES[t]
            rows = min(128, N - b * 128)
            si = b if t == 0 else rr_stgi.next()
            fns = []
            for kind, r0, poff, nr in segments(col0 + b * 128, rows):
                fns.append(lambda e, sm, kind=kind, r0=r0, poff=poff, nr=nr, si=si: e.dma_start(
                    out=stg_all[si][poff:poff + nr, :], in_=in_src[kind][r0:r0 + nr, :]).then_inc(sm, 16))
            P.dma("act" if (t == 0 or b >= 2) else "sp", fns, sem_stg_all[si], writes=[r_stg_all[si]])
            return (b, rows, si)

        def load_x_dma(t):
            col0, N = TILES[t]
            nblk = (N + 127) // 128
            return [load_x_dma_block(t, b) for b in range(min(4 if t == 0 else 2, nblk))]

        def load_x_transpose(t, blocks):
            col0, N = TILES[t]
            nblk = (N + 127) // 128
            blocks = list(blocks)
            bi = 0
            while bi < len(blocks):
                b, rows, si = blocks[bi]
                bi += 1
                for half in range(2):
                    bk = bank_t.next()
                    P.pe_group([(lambda e, cl=cl, bk=bk, rows=rows, si=si, half=half: e.transpose(
                        out=ps[bk][:, cl * 128:cl * 128 + rows],
                        in_=stg_all[si][0:rows, (half * 4 + cl) * 128:(half * 4 + cl + 1) * 128],
                        identity=ident32[0:rows, 0:rows]), [r_stg_all[si], r_ident32]) for cl in range(4)],
                        writes=[r_ps[bk]])
                    P.op("act", lambda e, bk=bk, rows=rows, half=half, b=b: e.activation(
                        out=xT[:, half * 4:(half + 1) * 4, b * 128:b * 128 + rows],
                        in_=ps[bk][:, :].rearrange("p (c n) -> p c n", n=128)[:, :, 0:rows], func=AF.Copy),
                        reads=[r_ps[bk]], writes=r_xT[half * 4:(half + 1) * 4])
                if len(blocks) < nblk:
                    blocks.append(load_x_dma_block(t, len(blocks)))

        def act_pre(func):
            P.op("act", lambda e: e.activation(out=dummy[:, 0:1], in_=epsc[:, 0:1], func=func), reads=[r_mhalf], writes=[r_dummy])

        def post_x(c, gcol, N):
            if gcol is not None:
                P.op("act", lambda e: e.activation(out=xn[:, c, 0:N], in_=xT[:, c, 0:N], func=AF.Identity, scale=cc(gcol + c)),
                     reads=[r_xT[c], r_consts], writes=[r_xn[c]])
            if c in (0, 7):
                P.op("act", lambda e: e.activation(out=sq_t[c][:, 0:N], in_=xT[:, c, 0:N], func=AF.Square),
                     reads=[r_xT[c]], writes=[r_sq[c]])
            elif c in (2, 3, 4):
                P.op("dve", lambda e: e.tensor_tensor(out=sq_t[c][:, 0:N], in0=xT[:, c, 0:N], in1=xT[:, c, 0:N], op=ALU.mult),
                     reads=[r_xT[c]], writes=[r_sq[c]])
            else:
                P.op("pool", lambda e: e.tensor_tensor(out=sq_t[c][:, 0:N], in0=xT[:, c, 0:N], in1=xT[:, c, 0:N], op=ALU.mult),
                     reads=[r_xT[c]], writes=[r_sq[c]])

        def stats_rstd(N):
            bk = bank_s.next()
            P.pe_group([(lambda e, c=c: e.matmul(ps[bk][:, 0:N], lhsT=ones_rms[:, :], rhs=sq_t[c][:, 0:N],
                                                 start=(c == 0), stop=(c == KC - 1)), [r_sq[c], r_ones]) for c in range(KC)],
                       writes=[r_ps[bk]])
            P.op("act", lambda e: e.activation(out=sst[:, 0:N], in_=ps[bk][:, 0:N], func=AF.Sqrt, bias=epsc[:, 0:1]),
                 reads=[r_ps[bk], r_mhalf], writes=[r_sst])
            act_pre(AF.Silu)
            P.op("dve", lambda e: e.reciprocal(out=rstd[:, 0:N], in_=sst[:, 0:N]), reads=[r_sst], writes=[r_rstd])

        def ffn(N, next_gcol):
            for j in range(JC):
                wt, wr = ring_gu.get()
                bg = bank_g.next()
                bu = bank_u.next()
                P.pe_group([(lambda e, kc=kc, wt=wt, bg=bg: e.matmul(ps[bg][:, 0:N], lhsT=wt[:, 0, kc, :], rhs=xn[:, kc, 0:N],
                                                                     start=(kc == 0), stop=(kc == KC - 1)), [wr, r_xn[kc]])
                            for kc in range(KC)], writes=[r_ps[bg]])
                P.pe_group([(lambda e, kc=kc, wt=wt, bu=bu: e.matmul(ps[bu][:, 0:N], lhsT=wt[:, 1, kc, :], rhs=xn[:, kc, 0:N],
                                                                     start=(kc == 0), stop=(kc == KC - 1)), [wr, r_xn[kc]])
                            for kc in range(KC)], writes=[r_ps[bu]])
                ring_gu.refill()
                si = rr_sg.next()
                P.op("dve", lambda e, si=si, bg=bg: e.tensor_tensor(out=sg_t[si][:, 0:N], in0=ps[bg][:, 0:N], in1=rstd[:, 0:N], op=ALU.mult),
                     reads=[r_ps[bg], r_rstd], writes=[r_sg[si]])
                P.op("act", lambda e, si=si: e.activation(out=sg_t[si][:, 0:N], in_=sg_t[si][:, 0:N], func=AF.Silu),
                     reads=[r_sg[si]], writes=[r_sg[si]])
                P.op("pool", lambda e, si=si: e.tensor_tensor(out=sg_t[si][:, 0:N], in0=sg_t[si][:, 0:N], in1=rstd[:, 0:N], op=ALU.mult),
                     reads=[r_sg[si], r_rstd], writes=[r_sg[si]])
                P.op("dve", lambda e, si=si, bu=bu, j=j: e.tensor_tensor(out=hbuf[:, j, 0:N], in0=ps[bu][:, 0:N],
                                                                         in1=sg_t[si][:, 0:N], op=ALU.mult),
                     reads=[r_ps[bu], r_sg[si]], writes=[r_h[j]])
                diag_some(2)
            act_pre(AF.Sqrt)
            for m in range(KC):
                wt, wr = ring_ds.get()
                bd = bank_d.next()
                P.pe_group([(lambda e, j=j, wt=wt, bd=bd: e.matmul(ps[bd][:, 0:N], lhsT=wt[:, j, :], rhs=hbuf[:, j, 0:N],
                                                                   start=(j == 0), stop=(j == JC - 1)), [wr, r_h[j]])
                            for j in range(JC)], writes=[r_ps[bd]])
                ring_ds.refill()
                P.op("dve", lambda e, m=m, bd=bd: e.scalar_tensor_tensor(
                    out=xT[:, m, 0:N], in0=ps[bd][:, 0:N], scalar=0.5, in1=xT[:, m, 0:N], op0=ALU.mult, op1=ALU.add),
                    reads=[r_ps[bd], r_xT[m]], writes=[r_xT[m]])
                if next_gcol is not None:
                    post_x(m, next_gcol, N)
                diag_some(10)
            diag_some(1000)
            if next_gcol is not None:
                stats_rstd(N)

        def mixer(t):
            col0, N = TILES[t]
            Np = min(N, PT - col0)
            Ns = N - Np
            first = (t == 0)
            last_p = (col0 + Np == PT)
            bm, bq = 6, 7
            st = {}

            def Z(c):
                wt, wr = ring_gu.get()
                ba = bank_g.next()
                bgt = bank_u.next()
                P.pe_group([(lambda e, kc=kc: e.matmul(ps[ba][:, 0:N], lhsT=wt[:, 0, kc, :], rhs=xn[:, kc, 0:N],
                                                       start=(kc == 0), stop=(kc == KC - 1)), [wr, r_xn[kc]])
                            for kc in range(KC)], writes=[r_ps[ba]])
                P.pe_group([(lambda e, kc=kc: e.matmul(ps[bgt][:, 0:N], lhsT=wt[:, 1, kc, :], rhs=xn[:, kc, 0:N],
                                                       start=(kc == 0), stop=(kc == KC - 1)), [wr, r_xn[kc]])
                            for kc in range(KC)], writes=[r_ps[bgt]])
                ring_gu.refill()
                ti = c % 2
                P.op("dve", lambda e: e.tensor_tensor(out=th_t[ti][:, 0:N], in0=ps[bgt][:, 0:N], in1=rstd[:, 0:N], op=ALU.mult),
                     reads=[r_ps[bgt], r_rstd], writes=[r_th[ti]])
                P.op("act", lambda e: e.activation(out=th_t[ti][:, 0:N], in_=th_t[ti][:, 0:N], func=AF.Tanh,
                                                   bias=hb[:, 4 + c:5 + c], scale=0.5), reads=[r_th[ti], r_hb], writes=[r_th[ti]])
                P.op("dve", lambda e: e.tensor_tensor(out=ah_t[ti][:, 0:N], in0=ps[ba][:, 0:N], in1=rstd[:, 0:N], op=ALU.mult),
                     reads=[r_ps[ba], r_rstd], writes=[r_ah[ti]])
                P.op("act", lambda e: e.activation(out=ah_t[ti][:, 0:N], in_=ah_t[ti][:, 0:N], func=AF.Identity,
                                                   bias=hb[:, c:c + 1], scale=0.5), reads=[r_ah[ti], r_hb], writes=[r_ah[ti]])
                P.op("dve", lambda e: e.scalar_tensor_tensor(
                    out=u32_t[ti][:, 0:N], in0=th_t[ti][:, 0:N], scalar=1.0, in1=ah_t[ti][:, 0:N], op0=ALU.add, op1=ALU.mult),
                    reads=[r_th[ti], r_ah[ti]], writes=[r_u32[ti]])
                P.op("act", lambda e: e.activation(out=ubf[:, c, CH:CH + Np], in_=u32_t[ti][:, 0:Np], func=AF.Copy),
                     reads=[r_u32[ti]], writes=[r_ubf[c]])
                if Ns:
                    P.op("act", lambda e: e.activation(
                        out=usbf[:, c, :, CH:CH + DSEQ], in_=u32_t[ti][:, Np:N].rearrange("p (s i) -> p s i", i=DSEQ),
                        func=AF.Copy), reads=[r_u32[ti]], writes=[r_usbf[c]])
                    P.op("dve", lambda e: e.tensor_copy(out=usnew[:, c, :], in_=u32_t[ti][:, Np:N]),
                         reads=[r_u32[ti]], writes=[r_usnew])
                if last_p:
                    P.op("dve", lambda e: e.tensor_copy(out=utail[:, c, :], in_=u32_t[ti][:, Np - CH:Np]),
                         reads=[r_u32[ti]], writes=[r_utail])

            def CONV(c):
                bc = bank_d.next()
                mm = [(lambda e, k=k: e.matmul(ps[bc][:, 0:Np], lhsT=diag[:, k * 4 + c, :], rhs=ubf[:, c, k:k + Np],
                                               start=(k == 0), stop=(k == CONV_W - 1)), [r_diag, r_ubf[c]])
                      for k in range(CONV_W)]
                if Ns:
                    mm += [(lambda e, k=k: e.matmul(
                        ps[bc][:, Np:N].rearrange("p (s i) -> p s i", i=DSEQ), lhsT=diag[:, k * 4 + c, :],
                        rhs=usbf[:, c, :, k:k + DSEQ], start=(k == 0), stop=(k == CONV_W - 1), skip_group_check=True),
                        [r_diag, r_usbf[c]]) for k in range(CONV_W)]
                P.pe_group(mm, writes=[r_ps[bc]])
                if not last_p:
                    P.op("act", lambda e: e.activation(out=ubf[:, c, 0:CH], in_=ubf[:, c, Np:Np + CH], func=AF.Copy),
                         reads=[r_ubf[c]], writes=[r_ubf[c]])
                bi = c % 2
                P.op("act", lambda e: e.activation(out=c32[:, c, 0:N], in_=ps[bc][:, 0:N], func=AF.Identity,
                                                   bias=cc(C_BDW + c)), reads=[r_ps[bc], r_consts], writes=[r_c32[c]])
                P.op("act", lambda e: e.activation(out=cbf_t[bi][:, 0:N], in_=ps[bc][:, 0:N], func=AF.Identity,
                                                   bias=cc(C_BDW + c)), reads=[r_ps[bc], r_consts], writes=[r_cbf[bi]])
                P.op("act", lambda e: e.activation(out=csq_t[bi][:, 0:N], in_=ps[bc][:, 0:N], func=AF.Square,
                                                   bias=cc(C_BDW + c)), reads=[r_ps[bc], r_consts], writes=[r_csq[bi]])

            def STAT(c):
                bi = c % 2
                P.pe_group([(lambda e: e.matmul(ps[bm][:, 0:N], lhsT=ones_ln[:, :], rhs=cbf_t[bi][:, 0:N],
                                                start=(c == 0), stop=(c == 3), skip_group_check=True), [r_cbf[bi], r_ones])],
                           writes=[r_ps[bm]] if c == 0 else [])
                st["m"] = ("pe", P.cnt["pe"])
                P.pe_group([(lambda e: e.matmul(ps[bq][:, 0:N], lhsT=ones_ln[:, :], rhs=csq_t[bi][:, 0:N],
                                                start=(c == 0), stop=(c == 3), skip_group_check=True), [r_csq[bi], r_ones])],
                           writes=[r_ps[bq]] if c == 0 else [])
                st["q"] = ("pe", P.cnt["pe"])
                if c == 3:
                    r_ps[bm].w = st["m"]
                    r_ps[bq].w = st["q"]

            def LN():
                P.op("act", lambda e: e.activation(out=ln_var[:, 0:N], in_=ps[bm][:, 0:N], func=AF.Square),
                     reads=[r_ps[bm]], writes=[r_lnv])
                P.op("dve", lambda e: e.scalar_tensor_tensor(out=ln_var[:, 0:N], in0=ps[bq][:, 0:N], scalar=EPS, in1=ln_var[:, 0:N],
                                                             op0=ALU.add, op1=ALU.subtract), reads=[r_ps[bq], r_lnv], writes=[r_lnv])
                P.op("act", lambda e: e.activation(out=ln_var[:, 0:N], in_=ln_var[:, 0:N], func=AF.Sqrt), reads=[r_lnv], writes=[r_lnv])
                act_pre(AF.Silu)

                def center(c):
                    P.op("dve", lambda e: e.tensor_tensor(out=c32[:, c, 0:N], in0=c32[:, c, 0:N], in1=ps[bm][:, 0:N], op=ALU.subtract),
                         reads=[r_c32[c], r_ps[bm]], writes=[r_c32[c]])

                def scale(c):
                    P.op("dve", lambda e: e.tensor_tensor(out=c32[:, c, 0:N], in0=c32[:, c, 0:N], in1=ln_rstd[:, 0:N], op=ALU.mult),
                         reads=[r_c32[c], r_lnr], writes=[r_c32[c]])
                    P.op("act", lambda e: e.activation(out=cq[:, c, 0:N], in_=c32[:, c, 0:N], func=AF.Silu,
                                                       bias=cc(C_LNB + c), scale=cc(C_LNG + c)),
                         reads=[r_c32[c], r_consts], writes=[r_cq[c]])

                center(0)
                P.op("dve", lambda e: e.reciprocal(out=ln_rstd[:, 0:N], in_=ln_var[:, 0:N]), reads=[r_lnv], writes=[r_lnr])
                scale(0)
                for c in range(1, 4):
                    center(c)
                    scale(c)

            def PZ(pair):
                wt, wr = ring_gu.get()
                for a in range(2):
                    g = pair * 2 + a
                    w = WINS[g]
                    if pair == 0:
                        bp = 6 if a == 0 else 7
                    else:
                        bp = bank_g.next() if a == 0 else bank_u.next()
                    P.pe_group([(lambda e, kc=kc, a=a, bp=bp: e.matmul(ps[bp][:, 0:N], lhsT=wt[:, a, kc, :], rhs=xn[:, kc, 0:N],
                                                                       start=(kc == 0), stop=(kc == KC - 1)), [wr, r_xn[kc]])
                                for kc in range(KC)], writes=[r_ps[bp]])
                    if a == 1:
                        ring_gu.refill()
                    P.op("dve", lambda e, g=g, bp=bp: e.tensor_tensor(out=pext[:, g, PH:PH + Np], in0=ps[bp][:, 0:Np], in1=rstd[:, 0:Np],
                                                                        op=ALU.mult), reads=[r_ps[bp], r_rstd], writes=[r_pext[g]])
                    P.op("act", lambda e, g=g: e.activation(out=pext[:, g, PH:PH + Np], in_=pext[:, g, PH:PH + Np], func=AF.Identity,
                                                            bias=cc(C_BIN + 8 + g)), reads=[r_pext[g], r_consts], writes=[r_pext[g]])
                    if Ns:
                        P.op("dve", lambda e, g=g, bp=bp: e.tensor_tensor(
                            out=psext[:, g, :, PH:PH + DSEQ], in0=ps[bp][:, Np:N].rearrange("p (s i) -> p s i", i=DSEQ),
                            in1=rstd[:, Np:N].rearrange("p (s i) -> p s i", i=DSEQ), op=ALU.mult),
                            reads=[r_ps[bp], r_rstd], writes=[r_psext[g]])
                        P.op("act", lambda e, g=g: e.activation(
                            out=psext[:, g, :, PH:PH + DSEQ], in_=psext[:, g, :, PH:PH + DSEQ],
                            func=AF.Identity, bias=cc(C_BIN + 8 + g)), reads=[r_psext[g], r_consts], writes=[r_psext[g]])
                    di = g
                    L = PH + Np
                    src, rsrc = pext[:, g, 0:L], r_pext[g]
                    bufs = [(sA, r_sA), (sB, r_sB)]
                    step = 1
                    bi = 0
                    while step < w:
                        dst, rdst = bufs[bi]
                        P.op("pool", lambda e, src=src, dst=dst, step=step, L=L: e.tensor_tensor(
                            out=dst[:, step:L], in0=src[:, step:L], in1=src[:, 0:L - step], op=ALU.add),
                            reads=[rsrc], writes=[rdst])
                        src, rsrc = dst[:, 0:L], rdst
                        step *= 2
                        bi ^= 1
                    P.op("dve", lambda e, src=src, g=g, di=di, w=w: e.scalar_tensor_tensor(
                        out=dd_t[di][:, 0:Np], in0=src[:, PH:PH + Np], scalar=1.0 / w, in1=pext[:, g, PH:PH + Np],
                        op0=ALU.mult, op1=ALU.subtract), reads=[rsrc, r_pext[g]], writes=[r_dd[di]])
                    if first:
                        P.op("dve", lambda e, src=src, g=g: e.tensor_tensor(
                            out=sst[:, 0:16], in0=src[:, PH:PH + 16], in1=invc[:, g * 16:(g + 1) * 16], op=ALU.mult),
                            reads=[rsrc, r_invc], writes=[r_sst])
                        P.op("dve", lambda e, g=g, di=di: e.tensor_tensor(
                            out=dd_t[di][:, 0:16], in0=sst[:, 0:16], in1=pext[:, g, PH:PH + 16], op=ALU.subtract),
                            reads=[r_sst, r_pext[g]], writes=[r_dd[di]])
                    if Ns:
                        Ls = PH + DSEQ
                        src2, rsrc2 = psext[:, g, :, :], r_psext[g]
                        bufs2 = [(sAs, r_sAs), (sBs, r_sBs)]
                        step = 1
                        bi = 0
                        while step < w:
                            dst2, rdst2 = bufs2[bi]
                            P.op("pool", lambda e, src2=src2, dst2=dst2, step=step, Ls=Ls: e.tensor_tensor(
                                out=dst2[:, :, step:Ls], in0=src2[:, :, step:Ls], in1=src2[:, :, 0:Ls - step], op=ALU.add),
                                reads=[rsrc2], writes=[rdst2])
                            src2, rsrc2 = dst2[:, :, :], rdst2
                            step *= 2
                            bi ^= 1
                        P.op("dve", lambda e, src2=src2, g=g, di=di, w=w: e.scalar_tensor_tensor(
                            out=dd_t[di][:, Np:N].rearrange("p (s i) -> p s i", i=DSEQ), in0=src2[:, :, PH:PH + DSEQ],
                            scalar=1.0 / w, in1=psext[:, g, :, PH:PH + DSEQ], op0=ALU.mult, op1=ALU.subtract),
                            reads=[rsrc2, r_psext[g]], writes=[r_dd[di]])
                        P.op("dve", lambda e, g=g: e.tensor_copy(
                            out=psnew[:, g, :].rearrange("p (s i) -> p s i", i=DSEQ), in_=psext[:, g, :, PH:PH + DSEQ]),
                            reads=[r_psext[g]], writes=[r_psnew])
                    if last_p:
                        P.op("dve", lambda e, g=g: e.tensor_copy(out=ptail[:, g, :], in_=pext[:, g, Np:Np + PH]),
                             reads=[r_pext[g]], writes=[r_ptail])
                    else:
                        P.op("pool", lambda e, g=g: e.tensor_copy(out=pext[:, g, 0:PH], in_=pext[:, g, Np:Np + PH]),
                             reads=[r_pext[g]], writes=[r_pext[g]])

            def Q(g):
                bqq = bank_d.next()
                P.pe_group([(lambda e: e.matmul(ps[bqq][:, 0:N], lhsT=wpool_b[:, g, :], rhs=dd_t[g][:, 0:N],
                                                start=True, stop=True), [r_wpb, r_dd[g]])], writes=[r_ps[bqq]])
                P.op("act", lambda e: e.activation(out=cq[:, 4 + g, 0:N], in_=ps[bqq][:, 0:N], func=AF.Identity,
                                                   scale=cc(C_PSC + g)), reads=[r_ps[bqq], r_consts], writes=[r_cq[4 + g]])

            def OUT():
                for pair in range(4):
                    wt, wr = ring_gu.get()
                    for a in range(2):
                        m = pair * 2 + a
                        bo = bank_d.next()
                        P.pe_group([(lambda e, k=k, a=a, bo=bo, wt=wt: e.matmul(ps[bo][:, 0:N], lhsT=wt[:, a, k, :], rhs=cq[:, k, 0:N],
                                                                         start=(k == 0), stop=(k == 7)), [wr, r_cq[k]])
                                    for k in range(8)], writes=[r_ps[bo]])
                        if a == 1:
                            ring_gu.refill()
                        P.op("dve", lambda e, m=m, bo=bo: e.scalar_tensor_tensor(
                            out=xT[:, m, 0:N], in0=ps[bo][:, 0:N], scalar=cc(C_BOUT + m), in1=xT[:, m, 0:N], op0=ALU.add, op1=ALU.add),
                            reads=[r_ps[bo], r_xT[m], r_consts], writes=[r_xT[m]])
                        post_x(m, C_G2, N)
                stats_rstd(N)

            Z(0)
            Z(1)
            PZ(0)
            CONV(0)
            Z(2)
            STAT(0)
            PZ(1)
            CONV(1)
            Z(3)
            act_pre(AF.Sqrt)
            STAT(1)
            CONV(2)
            for g in range(4):
                Q(g)
            STAT(2)
            CONV(3)
            STAT(3)
            LN()
            act_pre(AF.Sqrt)
            fb = bank_g.next()
            P.pe_group([(lambda e: e.matmul(ps[fb][:, 0:N], lhsT=ones_ln[:, :], rhs=xn[:, 0, 0:N], start=True, stop=True),
                         [r_ones, r_xn[0]]) for _ in range(24)], writes=[r_ps[fb]])
            OUT()

        out_toks = []

        bank_f = RR([7, 6, 0, 2, 1, 3])

        def final(t):
            col0, N = TILES[t]
            nblk = (N + 127) // 128
            for b in range(nblk):
                rows = min(128, N - b * 128)
                so = rr_stgo.next()
                bks = []
                for half in range(2):
                    bk = bank_f.next()
                    bks.append(bk)
                    P.pe_group([(lambda e, cl=cl, bk=bk, rows=rows, half=half, b=b: e.transpose(
                        out=ps[bk][0:rows, cl * 128:(cl + 1) * 128], in_=xT[:, half * 4 + cl, b * 128:b * 128 + rows],
                        identity=ident32[:, :]), [r_xT[half * 4 + cl], r_ident32]) for cl in range(4)], writes=[r_ps[bk]])
                    P.op("act", lambda e, bk=bk, rows=rows, half=half, so=so: e.activation(
                        out=stg_st[0:rows, :], in_=ps[bk][0:rows, :], func=AF.Square, accum_out=rcol[so][0:rows, half:half + 1]),
                        reads=[r_ps[bk]], writes=[r_stgst, r_rcol[so]])
                P.op("dve", lambda e, rows=rows, so=so: e.tensor_tensor(out=rcol[so][0:rows, 2:3], in0=rcol[so][0:rows, 0:1],
                                                                        in1=rcol[so][0:rows, 1:2], op=ALU.add),
                     reads=[r_rcol[so]], writes=[r_rcol[so]])
                P.op("act", lambda e, rows=rows, so=so: e.activation(out=rcol[so][0:rows, 3:4], in_=rcol[so][0:rows, 2:3], func=AF.Sqrt,
                                                                     bias=epsc[0:rows, 0:1], scale=1.0 / D),
                     reads=[r_rcol[so], r_mhalf], writes=[r_rcol[so]])
                P.op("dve", lambda e, rows=rows, so=so: e.reciprocal(out=rcol[so][0:rows, 4:5], in_=rcol[so][0:rows, 3:4]),
                     reads=[r_rcol[so]], writes=[r_rcol[so]])
                for half in range(2):
                    bk = bks[half]
                    P.op("dve", lambda e, bk=bk, rows=rows, half=half, so=so: e.scalar_tensor_tensor(
                        out=stg_out[so][0:rows, half * 512:(half + 1) * 512], in0=ps[bk][0:rows, :], scalar=rcol[so][0:rows, 4:5],
                        in1=gf_bc[0:rows, half * 512:(half + 1) * 512], op0=ALU.mult, op1=ALU.mult),
                        reads=[r_ps[bk], r_rcol[so], r_consts], writes=[r_stgo[so]])
                fns = []
                for kind, r0, poff, nr in segments(col0 + b * 128, rows):
                    if kind == "meta":
                        continue
                    fns.append(lambda e, sm, kind=kind, r0=r0, poff=poff, nr=nr, so=so: e.dma_start(
                        out=out_dst[kind][r0:r0 + nr, :], in_=stg_out[so][poff:poff + nr, :]).then_inc(sm, 16))
                if fns:
                    P.dma("sp", fns, sem_stgo[so], reads=[r_stgo[so]])

        def state_outputs():
            for (src, rs, rows, dst) in ((utail, r_utail, CH, ncp), (ptail, r_ptail, PH, npp)):
                bk = bank_t.next()
                P.pe_group([(lambda e, c=c, bk=bk, src=src, rows=rows: e.transpose(
                    out=ps[bk][0:rows, c * 128:(c + 1) * 128], in_=src[:, c, :], identity=ident32[:, :]), [rs, r_ident32])
                    for c in range(4)], writes=[r_ps[bk]])
                so = rr_so.next()
                P.op("act", lambda e, bk=bk, rows=rows, so=so: e.activation(out=stg_so[so][0:rows, :], in_=ps[bk][0:rows, :], func=AF.Copy),
                     reads=[r_ps[bk]], writes=[r_stgso[so]])
                P.dma("sp", [lambda e, sm, so=so, rows=rows, dst=dst: e.dma_start(out=dst[:, :], in_=stg_so[so][0:rows, :]).then_inc(sm, 16)],
                      sem_so[so], reads=[r_stgso[so]])
            for (src, rs, hist, dst) in ((usnew, r_usnew, CH, ncs), (psnew, r_psnew, PH, nps)):
                bk = bank_t.next()
                P.pe_group([(lambda e, c=c, bk=bk, src=src: e.transpose(
                    out=ps[bk][0:ST, c * 128:(c + 1) * 128], in_=src[:, c, :], identity=ident32[:, :]), [rs, r_ident32])
                    for c in range(4)], writes=[r_ps[bk]])
                so = rr_so.next()
                P.op("act", lambda e, bk=bk, so=so: e.activation(out=stg_so[so][0:ST, :], in_=ps[bk][0:ST, :], func=AF.Copy),
                     reads=[r_ps[bk]], writes=[r_stgso[so]])
                fns = []
                for s in range(NSS):
                    r0 = s * hist + hist - DSEQ
                    fns.append(lambda e, sm, s=s, r0=r0, so=so, dst=dst: e.dma_start(
                        out=dst[r0:r0 + DSEQ, :], in_=stg_so[so][s * DSEQ:(s + 1) * DSEQ, :]).then_inc(sm, 16))
                P.dma("sp", fns, sem_so[so], reads=[r_stgso[so]])

        blocks = load_x_dma(0)
        q0, f0, rd0, p0 = gu_loads[0]
        gu_loads[0] = (q0, f0, list(rd0) + [r_consts, r_stgi[0]], p0)
        ring_gu.refill()
        ring_ds.refill()
        for t in range(len(TILES)):
            col0, N = TILES[t]
            load_x_transpose(t, blocks)
            dump(t, 0)
            for c in range(KC):
                post_x(c, C_G1, N)
            stats_rstd(N)
            ffn(N, C_GM)
            if t == 0:
                load_states()
            dump(t, 1)
            mixer(t)
            if t == len(TILES) - 1:
                state_outputs()
            dump(t, 2)
            if t + 1 < len(TILES):
                blocks = load_x_dma(t + 1)
            ffn(N, None)
            dump(t, 3)
            final(t)
            dump(t, 4)
        assert ring_gu.consumed == len(gu_loads) and ring_ds.consumed == len(ds_loads)

        sem_names = list(Prog.QUEUES[:4]) + P.dma_sems
        sem_ctx = {}
        for name in sem_names:
            sem_ctx[name] = es.enter_context(nc.semaphore(name))
        final_waits = [(s, P.cnt[s]) for s in P.dma_sems if P.cnt[s] > 0 and (s.startswith("stgo") or s.startswith("stgi") or s.startswith("so") or s == "d2d" or s == "dbg")]
        block = es.enter_context(nc.Block())

        @block.tensor
        def _(e):
            P.replay("pe", e, sem_ctx)

        @block.scalar
        def _(e):
            P.replay("act", e, sem_ctx)

        @block.vector
        def _(e):
            P.replay("dve", e, sem_ctx)

        @block.gpsimd
        def _(e):
            P.replay("pool", e, sem_ctx)

        @block.sync
        def _(e):
            P.replay("sp", e, sem_ctx)
            for s, v in final_waits:
                e.wait_ge(sem_ctx[s], v)
    return nc


_CACHE = {}


def _consts_array(inp):
    c = np.zeros((128, NCONST), np.float32)

    def put(col, vec):
        v = np.asarray(vec, np.float32).reshape(-1, 128)
        c[:, col:col + v.shape[0]] = v.T

    put(C_G1, inp["norm_ffn1"][0])
    put(C_GM, inp["norm_mix"][0])
    put(C_G2, inp["norm_ffn2"][0])
    put(C_GF, inp["norm_final"])
    put(C_BIN, inp["b_in"][0])
    put(C_BDW, inp["b_dw"][0])
    put(C_LNG, inp["ln_conv_g"][0])
    put(C_LNB, inp["ln_conv_b"][0])
    put(C_PSC, inp["pool_scale"][0])
    put(C_BOUT, inp["b_out"][0])
    wdw = np.asarray(inp["w_dw"][0], np.float32)
    for k in range(CONV_W):
        put(C_WDW + k * 4, wdw[k])
    return c


def _gu_layout(wa, wb, cols_a, cols_b):
    n = len(cols_a)
    out = np.empty((n, 128, 2, KC, 128), np.float32)
    wa4 = wa.reshape(KC, 128, -1, 128)
    wb4 = wb.reshape(KC, 128, -1, 128)
    for i in range(n):
        out[i, :, 0] = wa4[:, :, cols_a[i], :].transpose(1, 0, 2)
        out[i, :, 1] = wb4[:, :, cols_b[i], :].transpose(1, 0, 2)
    return out.reshape(n, 128, 2 * KC * 128)


def _d_layout(wdn):
    w4 = wdn.reshape(JC, 128, KC, 128)
    return np.ascontiguousarray(w4.transpose(2, 1, 0, 3)).reshape(KC, 128, JC * 128)


def kernel(**inp):
    inp = {k: np.asarray(v) for k, v in inp.items()}
    if "nc" not in _CACHE:
        _CACHE["nc"] = build_program()
    nc = _CACHE["nc"]

    w1g, w1u, w1d = inp["w_ffn1_gate"][0], inp["w_ffn1_up"][0], inp["w_ffn1_down"][0]
    w2g, w2u, w2d = inp["w_ffn2_gate"][0], inp["w_ffn2_up"][0], inp["w_ffn2_down"][0]
    w_in, w_out = inp["w_in"][0], inp["w_out"][0]
    shared = {
        "wgu1": _gu_layout(w1g, w1u, list(range(JC)), list(range(JC))),
        "wgu2": _gu_layout(w2g, w2u, list(range(JC)), list(range(JC))),
        "wd1": _d_layout(w1d),
        "wd2": _d_layout(w2d),
        "win": _gu_layout(w_in, w_in, [0, 1, 2, 3, 8, 10], [4, 5, 6, 7, 9, 11]),
        "wout": _gu_layout(w_out, w_out, [0, 2, 4, 6], [1, 3, 5, 7]),
        "wpool": np.ascontiguousarray(inp["w_pool"][0].transpose(1, 0, 2)).reshape(128, 512).astype(np.float32),
        "consts": _consts_array(inp),
        "ident": np.eye(128, dtype=np.float32),
        "meta": np.ascontiguousarray(inp["meta_tokens"], np.float32),
    }
    invc = np.zeros((128, 64), np.float32)
    for g, w in enumerate(WINS):
        invc[:, g * 16:(g + 1) * 16] = 1.0 / np.minimum(np.arange(16) + 1, w).astype(np.float32)
    shared["invc"] = invc
    shared["gfbc"] = np.ascontiguousarray(np.broadcast_to(np.asarray(inp["norm_final"], np.float32)[None, :], (128, D)))

    in_maps = []
    for c in range(8):
        m = dict(shared)
        m["xp"] = np.ascontiguousarray(inp["x_prompt"][c], np.float32)
        m["xs"] = np.ascontiguousarray(inp["x_sample"][c * NSS:(c + 1) * NSS].reshape(ST, D), np.float32)
        m["sconv"] = np.ascontiguousarray(inp["state_conv"][0, c * NSS:(c + 1) * NSS].reshape(NSS * CH, CC), np.float32)
        m["spool"] = np.ascontiguousarray(inp["state_pool"][0, c * NSS:(c + 1) * NSS].reshape(NSS * PH, CC), np.float32)
        in_maps.append(m)
    res = run_bass_kernel_spmd(nc, in_maps, core_ids=list(range(8)))
    rs = res.results
    _CACHE["last"] = rs
    y_prompt = np.stack([rs[c]["yp"] for c in range(8)], 0)
    y_sample = np.concatenate([rs[c]["ys"].reshape(NSS, DSEQ, D) for c in range(8)], 0)
    ncp_o = np.stack([rs[c]["ncp"] for c in range(8)], 0)[None]
    npp_o = np.stack([rs[c]["npp"] for c in range(8)], 0)[None]
    ncs_o = np.concatenate([rs[c]["ncs"].reshape(NSS, CH, CC) for c in range(8)], 0)[None]
    nps_o = np.concatenate([rs[c]["nps"].reshape(NSS, PH, CC) for c in range(8)], 0)[None]
    return (y_prompt.astype(np.float32), y_sample.astype(np.float32), ncp_o.astype(np.float32),
            npp_o.astype(np.float32), ncs_o.astype(np.float32), nps_o.astype(np.float32))
```

```python
import numpy as np
import concourse.bass as bass
import concourse.mybir as mybir
from concourse.bass_utils import run_bass_kernel_spmd

F32 = mybir.dt.float32
BF16 = mybir.dt.bfloat16
AF = mybir.ActivationFunctionType
ALU = mybir.AluOpType

D = 1024
KC = 8
DFF = 2816
JC = 22
CC = 512
NMETA = 16
SEQ = 2048
PT = NMETA + SEQ
NSS = 16
DSEQ = 4
ST = NSS * DSEQ
T = PT + ST
CONV_W = 31
CH = 30
PH = 15
WINS = (2, 4, 8, 16)
EPS = 1e-6
TILES = [(0, 432), (432, 432), (864, 432), (1296, 432), (1728, 400)]
NMAX = 432

C_G1, C_GM, C_G2, C_GF, C_BIN, C_BDW, C_LNG, C_LNB, C_PSC, C_BOUT, C_WDW = 0, 8, 16, 24, 32, 44, 48, 52, 56, 60, 68
NCONST = C_WDW + CONV_W * 4

N_GU = 6
N_DS = 4
WEIGHT_MODE = "stream32"


class Res:
    __slots__ = ("w", "r", "name")

    def __init__(self, name=""):
        self.w = None
        self.r = []
        self.name = name


class Prog:
    QUEUES = ("pe", "act", "dve", "pool", "sp")

    def __init__(self):
        self.q = {k: [] for k in self.QUEUES}
        self.cnt = {k: 0 for k in self.QUEUES}
        self.waited = {k: {} for k in self.QUEUES}
        self.dma_sems = []

    def new_dma_sem(self, name):
        self.cnt[name] = 0
        self.dma_sems.append(name)
        return name

    def _waits(self, q, deps):
        out = []
        for s, v in deps.items():
            if q == "pe" and s == "pe":
                continue
            if self.waited[q].get(s, 0) >= v:
                continue
            self.waited[q][s] = v
            out.append((s, v))
        return out

    @staticmethod
    def _flat(rs):
        out = []
        for r in rs:
            if isinstance(r, (list, tuple)):
                out.extend(Prog._flat(r))
            else:
                out.append(r)
        return out

    @staticmethod
    def _add(deps, tok):
        if tok is None:
            return
        s, v = tok
        if deps.get(s, 0) < v:
            deps[s] = v

    def op(self, q, fn, reads=(), writes=()):
        reads = self._flat(reads)
        writes = self._flat(writes)
        deps = {}
        for r in reads:
            self._add(deps, r.w)
        for r in writes:
            self._add(deps, r.w)
            for t in r.r:
                self._add(deps, t)
        waits = self._waits(q, deps)
        self.cnt[q] += 1
        tok = (q, self.cnt[q])
        self.q[q].append((waits, fn, True))
        for r in reads:
            r.r.append(tok)
        for r in writes:
            r.w = tok
            r.r = []
        return tok

    def pe_group(self, mms, writes):
        n = len(mms)
        allreads = []
        writes = self._flat(writes)
        for i, (fn, reads) in enumerate(mms):
            reads = self._flat(reads)
            deps = {}
            for r in reads:
                self._add(deps, r.w)
            if i == 0:
                for r in writes:
                    self._add(deps, r.w)
                    for t in r.r:
                        self._add(deps, t)
            waits = self._waits("pe", deps)
            self.q["pe"].append((waits, fn, i == n - 1))
            allreads.extend(reads)
        self.cnt["pe"] += 1
        tok = ("pe", self.cnt["pe"])
        seen = set()
        for r in allreads:
            if id(r) in seen:
                continue
            seen.add(id(r))
            r.r.append(tok)
        for r in writes:
            r.w = tok
            r.r = []
        return tok

    def dma(self, q, fns, sem, reads=(), writes=()):
        reads = self._flat(reads)
        writes = self._flat(writes)
        deps = {}
        for r in reads:
            self._add(deps, r.w)
        for r in writes:
            self._add(deps, r.w)
            for t in r.r:
                self._add(deps, t)
        waits = self._waits(q, deps)
        for i, fn in enumerate(fns):
            self.q[q].append((waits if i == 0 else [], (fn, sem), None))
            self.cnt[sem] += 16
        tok = (sem, self.cnt[sem])
        for r in reads:
            r.r.append(tok)
        for r in writes:
            r.w = tok
            r.r = []
        return tok

    def replay(self, q, eng, sems):
        for waits, fn, inc in self.q[q]:
            for s, v in waits:
                eng.wait_ge(sems[s], v)
            if inc is None:
                f, sem = fn
                f(eng, sems[sem])
            else:
                ins = fn(eng)
                if inc:
                    ins.then_inc(sems[q], 1)


class Ring:
    def __init__(self, P, name, nslots, tensors, loads):
        self.P = P
        self.n = nslots
        self.t = tensors
        self.res = [Res(f"{name}{i}") for i in range(nslots)]
        self.sems = {"sp": [P.new_dma_sem(f"{name}{i}") for i in range(nslots)],
                     "pool": [P.new_dma_sem(f"{name}c{i}") for i in range(nslots)]}
        self.loads = loads
        self.emitted = 0
        self.consumed = 0

    def refill(self):
        lim = min(len(self.loads), self.consumed + self.n)
        while self.emitted < lim:
            i = self.emitted
            s = i % self.n
            queue, ld, rd, post = self.loads[i]
            dst = self.t[s]
            self.P.dma(queue, [lambda e, sm, ld=ld, dst=dst: ld(e, sm, dst)], self.sems[queue][s], reads=rd, writes=[self.res[s]])
            if post is not None:
                post(dst, self.res[s])
            self.emitted += 1

    def get(self):
        self.refill()
        i = self.consumed
        assert i < self.emitted, "ring underflow"
        self.consumed += 1
        s = i % self.n
        return self.t[s], self.res[s]


def build_program():
    nc = bass.Bass("TRN2", target_bir_lowering=False)
    P = Prog()

    def din(name, shape):
        return nc.dram_tensor(name, list(shape), F32, kind="ExternalInput").ap()

    def dout(name, shape):
        return nc.dram_tensor(name, list(shape), F32, kind="ExternalOutput").ap()

    xp = din("xp", (SEQ, D))
    xs = din("xs", (ST, D))
    meta = din("meta", (NMETA, D))
    sconv = din("sconv", (NSS * CH, CC))
    spool = din("spool", (NSS * PH, CC))
    wgu = [din("wgu1", (JC, 128, 2 * KC * 128)), din("wgu2", (JC, 128, 2 * KC * 128))]
    wd = [din("wd1", (KC, 128, JC * 128)), din("wd2", (KC, 128, JC * 128))]
    win = din("win", (6, 128, 2 * KC * 128))
    wout = din("wout", (4, 128, 2 * KC * 128))
    wpool = din("wpool", (128, 4 * 128))
    consts_d = din("consts", (128, NCONST))
    ident_d = din("ident", (128, 128))
    invc_d = din("invc", (128, 4 * 16))
    gfbc_d = din("gfbc", (128, D))

    yp = dout("yp", (SEQ, D))
    ys = dout("ys", (ST, D))
    ncp = dout("ncp", (CH, CC))
    npp = dout("npp", (PH, CC))
    ncs = dout("ncs", (NSS * CH, CC))
    nps = dout("nps", (NSS * PH, CC))

    import os
    DBG = int(os.environ.get("KDBG", "-1"))
    dbg_d = [dout(f"dbg{i}", (128, KC * NMAX)) for i in range(5)] if DBG >= 0 else None
    from contextlib import ExitStack
    es = ExitStack()

    def sb(name, shape, dt=F32):
        return es.enter_context(nc.sbuf_tensor("sb_" + name, list(shape), dt))

    def pst(name):
        return es.enter_context(nc.psum_tensor(name, [128, 512], F32))

    with es:
        consts = sb("consts", (128, NCONST))
        hb = sb("hb", (128, 8))
        ident32 = sb("ident32", (128, 128))
        ones_rms = sb("ones_rms", (128, 128), BF16)
        ones_ln = sb("ones_ln", (128, 128), BF16)
        invc = sb("invc", (128, 4 * 16))
        gf_bc = sb("gf_bc", (128, D))
        ones32 = sb("ones32", (128, 8))
        rcol = [sb(f"rcol{i}", (128, 8)) for i in range(3)]
        mhalf = sb("mhalf", (128, 8))
        dummy = sb("dummy", (128, 8))
        epsc = mhalf
        diag = sb("diag", (128, CONV_W * 4, 128), BF16)
        wpool_b = sb("wpool_b", (128, 4, 128), BF16)

        xT = sb("xT", (128, KC, NMAX))
        xn = sb("xn", (128, KC, NMAX), BF16)
        hbuf = sb("hbuf", (128, JC, NMAX), BF16)
        gu_t = [sb(f"gu{i}", (128, 2, KC, 128), BF16) for i in range(N_GU)]
        ds_t = [sb(f"ds{i}", (128, JC, 128), BF16) for i in range(N_DS)]
        sg_t = [sb(f"sg{i}", (128, NMAX)) for i in range(3)]
        sq_t = [sb(f"sq{i}", (128, NMAX), BF16) for i in range(8)]
        sst = sb("sst", (128, NMAX))
        rstd = sb("rstd", (128, NMAX))
        stg_in = [sb(f"stgi{i}", (128, D)) for i in range(2)]
        stg_out = [sb(f"stgo{i}", (128, D)) for i in range(3)]
        th_t = [sb(f"th{i}", (128, NMAX)) for i in range(2)]
        ah_t = [sb(f"ah{i}", (128, NMAX)) for i in range(2)]
        u32_t = [sb(f"u32{i}", (128, NMAX)) for i in range(2)]
        ubf = sb("ubf", (128, 4, CH + NMAX), BF16)
        usbf = sb("usbf", (128, 4, NSS, CH + DSEQ), BF16)
        utail = sb("utail", (128, 4, CH))
        usnew = sb("usnew", (128, 4, ST))
        c32 = hbuf[:, 0:8, :].rearrange("p j n -> p (j n)").bitcast(F32).rearrange("p (c n) -> p c n", c=4)
        cbf_t = [sb(f"cbf{i}", (128, NMAX), BF16) for i in range(2)]
        csq_t = [sb(f"csq{i}", (128, NMAX), BF16) for i in range(2)]
        ln_mean = hbuf[:, 16:18, :].rearrange("p j n -> p (j n)").bitcast(F32)
        ln_var = hbuf[:, 18:20, :].rearrange("p j n -> p (j n)").bitcast(F32)
        ln_rstd = sb("ln_rstd", (128, NMAX))
        cq = hbuf[:, 8:16, :]
        pext = sb("pext", (128, 4, PH + NMAX))
        psext = sb("psext", (128, 4, NSS, PH + DSEQ))
        sA = sb("sA", (128, PH + NMAX))
        sB = sb("sB", (128, PH + NMAX))
        sAs = sA[:, 0:NSS * (PH + DSEQ)].rearrange("p (s r) -> p s r", r=PH + DSEQ)
        sBs = sB[:, 0:NSS * (PH + DSEQ)].rearrange("p (s r) -> p s r", r=PH + DSEQ)
        dd_t = [sb(f"dd{i}", (128, NMAX), BF16) for i in range(4)]
        ptail = sb("ptail", (128, 4, PH))
        psnew = sb("psnew", (128, 4, ST))
        stg_st = hbuf[:, 0:3, :].rearrange("p j n -> p (j n)").bitcast(F32)[:, 0:CC]
        wpool_f = stg_st
        stg_so = [stg_out[i][:, 0:CC] for i in range(2)]

        ps = [pst(f"ps{i}") for i in range(8)]

        R = Res
        r_consts, r_hb, r_ident32, r_identb, r_ones, r_invc, r_diag, r_wpf, r_wpb = (R() for _ in range(9))
        r_mhalf = R()
        r_dummy = R()
        r_rcol = [R(), R(), R()]
        r_xT = [R(f"xT{c}") for c in range(KC)]
        r_xn = [R(f"xn{c}") for c in range(KC)]
        r_h = [R(f"h{j}") for j in range(JC)]
        r_sg = [R() for _ in range(3)]
        r_sq = [R() for _ in range(8)]
        r_sst, r_rstd, r_rscr = R(), R(), R()
        r_stgi = [R(), R()]
        r_stgo = [R(), R(), R()]
        r_ps = [R(f"ps{i}") for i in range(8)]
        r_th = [R(), R()]
        r_ah = [R(), R()]
        r_u32 = [R(), R()]
        r_ubf = [R() for _ in range(4)]
        r_usbf = [R() for _ in range(4)]
        r_utail, r_usnew = R(), R()
        r_c32 = [[r_h[2 * c], r_h[2 * c + 1]] for c in range(4)]
        r_cbf = [R(), R()]
        r_csq = [R(), R()]
        r_lnm, r_lnv, r_lnr, r_lnmr = [r_h[16], r_h[17]], [r_h[18], r_h[19]], R(), R()
        r_cq = [r_h[8 + k] for k in range(8)]
        r_pext = [R() for _ in range(4)]
        r_psext = [R() for _ in range(4)]
        r_sA, r_sB = R(), R()
        r_sAs, r_sBs = r_sA, r_sB
        r_dd = [R() for _ in range(4)]
        r_ptail, r_psnew = R(), R()
        r_stgst = [r_h[0], r_h[1], r_h[2]]
        r_wpf = r_stgst
        r_stgso = r_stgo

        sem_misc = P.new_dma_sem("misc")
        sem_stgi = [P.new_dma_sem("stgi0"), P.new_dma_sem("stgi1")]
        sem_stgo = [P.new_dma_sem("stgo0"), P.new_dma_sem("stgo1"), P.new_dma_sem("stgo2")]
        sem_st = P.new_dma_sem("st")
        sem_so = [P.new_dma_sem(f"so{i}") for i in range(2)]
        sem_d2d = P.new_dma_sem("d2d")
        sem_dbg = P.new_dma_sem("dbg")

        def dump(t, i):
            if DBG == t:
                P.dma("sp", [lambda e, sm, i=i: e.dma_start(out=dbg_d[i][:, :], in_=xT[:, :, :].rearrange("p c n -> p (c n)")).then_inc(sm, 16)],
                      sem_dbg, reads=r_xT)

        class RR:
            def __init__(self, items):
                self.items = items
                self.i = 0

            def next(self):
                v = self.items[self.i % len(self.items)]
                self.i += 1
                return v

        bank_g = RR([0, 1])
        bank_u = RR([2, 3])
        bank_d = RR([4, 5])
        bank_s = RR([6])
        bank_t = RR([7, 6])
        bank_f = RR([7, 6, 0, 2, 1, 3])
        rr_sg = RR([0, 1, 2])
        rr_sq = RR(list(range(8)))
        rr_stgi = RR([0, 1])
        rr_stgo = RR([0, 1, 2])
        rr2 = {k: RR([0, 1]) for k in ("th", "ah", "u32", "cbf", "csq", "dd")}
        rr_so = RR([0, 1])

        wgu_b = [nc.dram_tensor("wgu1_b", [JC, 128, 2 * KC * 128], BF16).ap(), nc.dram_tensor("wgu2_b", [JC, 128, 2 * KC * 128], BF16).ap()]
        wd_b = [nc.dram_tensor("wd1_b", [KC, 128, JC * 128], BF16).ap(), nc.dram_tensor("wd2_b", [KC, 128, JC * 128], BF16).ap()]
        win_b = nc.dram_tensor("win_b", [6, 128, 2 * KC * 128], BF16).ap()
        wout_b = nc.dram_tensor("wout_b", [4, 128, 2 * KC * 128], BF16).ap()
        NPC = 8
        pc_sems = [P.new_dma_sem(f"pc{i}") for i in range(NPC)]
        pc_slot_res = [Res() for _ in range(NPC)]
        pc_state = {"n": 0}
        chunk_res = {}

        def mk_loads(src32, srcb, i, kind, maxlast):
            key = (id(srcb), i)
            chunk_res[key] = Res()
            flat = (lambda dst: dst[:, :, :, :].rearrange("p a k n -> p (a k n)")) if kind == "gu" else \
                   (lambda dst: dst[:, :, :].rearrange("p j n -> p (j n)"))

            def ld0(e, sm, dst):
                e.dma_start(out=flat(dst), in_=src32[i, :, :], max_dma_last_dim=maxlast).then_inc(sm, 16)

            def post0(dst, slot_res):
                sl = pc_state["n"] % NPC
                pc_state["n"] += 1
                P.dma("sp", [lambda e, sm: e.dma_start(out=srcb[i, :, :], in_=flat(dst)).then_inc(sm, 16)], pc_sems[sl],
                      reads=[slot_res], writes=[chunk_res[key], pc_slot_res[sl]])

            def ld1(e, sm, dst):
                e.dma_start(out=flat(dst), in_=srcb[i, :, :]).then_inc(sm, 16)

            return ("pool", ld0, [], post0), ("sp", ld1, [chunk_res[key]], None), ("pool", ld0, [], None)

        gu_seq = ([(wgu[0], wgu_b[0], j) for j in range(JC)] + [(win, win_b, i) for i in (0, 1, 4, 2, 5, 3)] +
                  [(wout, wout_b, i) for i in range(4)] + [(wgu[1], wgu_b[1], j) for j in range(JC)])
        ds_seq = [(wd[0], wd_b[0], m) for m in range(KC)] + [(wd[1], wd_b[1], m) for m in range(KC)]
        gu_pairs = [mk_loads(a_, b_, i, "gu", 8192) for (a_, b_, i) in gu_seq]
        ds_pairs = [mk_loads(a_, b_, i, "ds", 5632) for (a_, b_, i) in ds_seq]
        def seq_loads(pairs):
            out = []
            out += [p[0] if i % 2 == 0 else p[2] for i, p in enumerate(pairs)]
            out += [p[1] if i % 2 == 0 else p[0] for i, p in enumerate(pairs)]
            for _t in range(2, len(TILES)):
                out += [p[1] for p in pairs]
            return out
        gu_loads = seq_loads(gu_pairs)
        ds_loads = seq_loads(ds_pairs)
        if WEIGHT_MODE == "stream32":
            gu_loads = [p[2] for p in gu_pairs] * len(TILES)
            ds_loads = [p[2] for p in ds_pairs] * len(TILES)
        ring_gu = Ring(P, "rgu", N_GU, gu_t, gu_loads)
        ring_ds = Ring(P, "rds", N_DS, ds_t, ds_loads)

        def cc(col, n=1):
            return consts[:, col:col + n]

        P.dma("sp", [lambda e, sm: e.dma_start(out=consts[:, :], in_=consts_d[:, :]).then_inc(sm, 16),
                     lambda e, sm: e.dma_start(out=ident32[:, :], in_=ident_d[:, :]).then_inc(sm, 16),
                     lambda e, sm: e.dma_start(out=invc[:, :], in_=invc_d[:, :]).then_inc(sm, 16),
                     lambda e, sm: e.dma_start(out=wpool_f[:, :], in_=wpool[:, :]).then_inc(sm, 16),
                     lambda e, sm: e.dma_start(out=gf_bc[:, :], in_=gfbc_d[:, :]).then_inc(sm, 16)],
              sem_misc, writes=[r_consts, r_ident32, r_invc, r_wpf])

        P.op("dve", lambda e: e.memset(mhalf[:, :], EPS), writes=[r_mhalf])
        P.op("dve", lambda e: e.memset(ones32[:, :], 1.0), writes=[r_ones])
        P.op("dve", lambda e: e.memset(ones_rms[:, :], 1.0 / D), writes=[r_ones])
        P.op("dve", lambda e: e.memset(ones_ln[:, :], 1.0 / CC), writes=[r_ones])
        P.op("dve", lambda e: e.tensor_scalar(out=hb[:, :], in0=consts[:, C_BIN:C_BIN + 8], scalar1=0.5, scalar2=None,
                                              op0=ALU.mult), reads=[r_consts], writes=[r_hb])
        P.op("dve", lambda e: e.tensor_copy(out=wpool_b[:, :, :].rearrange("p g n -> p (g n)"), in_=wpool_f[:, :]),
             reads=[r_wpf], writes=[r_wpb])
        P.op("pool", lambda e: e.memset(sA[:, :], 0.0), writes=[r_sA])
        P.op("pool", lambda e: e.memset(sB[:, :], 0.0), writes=[r_sB])
        P.op("pool", lambda e: e.memset(sAs[:, :, :], 0.0), writes=[r_sAs])
        P.op("pool", lambda e: e.memset(sBs[:, :, :], 0.0), writes=[r_sBs])
        P.op("dve", lambda e: e.memset(ubf[:, :, 0:CH], 0.0), writes=r_ubf)
        P.op("dve", lambda e: e.memset(pext[:, :, 0:PH], 0.0), writes=r_pext)
        diag_todo = [(k, c) for k in range(CONV_W) for c in range(4)]

        def diag_some(n):
            for _ in range(min(n, len(diag_todo))):
                k, c = diag_todo.pop(0)
                P.op("dve", lambda e, k=k, c=c: e.tensor_scalar(
                    out=diag[:, k * 4 + c, :], in0=ident32[:, :], scalar1=cc(C_WDW + k * 4 + c), scalar2=None,
                    op0=ALU.mult), reads=[r_ident32, r_consts], writes=[r_diag])

        def load_states():
          for blk in range(4):
              P.dma("sp", [lambda e, sm, blk=blk: e.dma_start(out=stg_st[0:120, :], in_=sconv[blk * 120:(blk + 1) * 120, :]
                                                               ).then_inc(sm, 16)], sem_st, writes=[r_stgst])
              b = bank_t.next()
              P.pe_group([(lambda e, c=c, b=b: e.transpose(out=ps[b][:, c * 128:c * 128 + 120],
                                                           in_=stg_st[0:120, c * 128:(c + 1) * 128],
                                                           identity=ident32[0:120, 0:120]), [r_stgst, r_ident32])
                          for c in range(4)], writes=[r_ps[b]])
              for c in range(4):
                  P.op("act", lambda e, c=c, b=b, blk=blk: e.activation(
                      out=usbf[:, c, blk * 4:(blk + 1) * 4, 0:CH],
                      in_=ps[b][:, c * 128:c * 128 + 120].rearrange("p (s r) -> p s r", r=CH), func=AF.Copy),
                      reads=[r_ps[b]], writes=[r_usbf[c]])
          for blk in range(2):
              P.dma("sp", [lambda e, sm, blk=blk: e.dma_start(out=stg_st[0:120, :], in_=spool[blk * 120:(blk + 1) * 120, :]
                                                               ).then_inc(sm, 16)], sem_st, writes=[r_stgst])
              b = bank_t.next()
              P.pe_group([(lambda e, c=c, b=b: e.transpose(out=ps[b][:, c * 128:c * 128 + 120],
                                                           in_=stg_st[0:120, c * 128:(c + 1) * 128],
                                                           identity=ident32[0:120, 0:120]), [r_stgst, r_ident32])
                          for c in range(4)], writes=[r_ps[b]])
              for c in range(4):
                  P.op("act", lambda e, c=c, b=b, blk=blk: e.activation(
                      out=psext[:, c, blk * 8:(blk + 1) * 8, 0:PH],
                      in_=ps[b][:, c * 128:c * 128 + 120].rearrange("p (s r) -> p s r", r=PH), func=AF.Copy),
                      reads=[r_ps[b]], writes=[r_psext[c]])

        P.dma("sp", [lambda e, sm: e.dma_start(
            out=ncs.rearrange("(s r) c -> s (r c)", r=CH)[:, 0:(CH - DSEQ) * CC],
            in_=sconv.rearrange("(s r) c -> s (r c)", r=CH)[:, DSEQ * CC:CH * CC]).then_inc(sm, 16),
            lambda e, sm: e.dma_start(
            out=nps.rearrange("(s r) c -> s (r c)", r=PH)[:, 0:(PH - DSEQ) * CC],
            in_=spool.rearrange("(s r) c -> s (r c)", r=PH)[:, DSEQ * CC:PH * CC]).then_inc(sm, 16)], sem_d2d)

        def segments(col, n):
            out = []
            c = col
            end = col + n
            while c < end:
                if c < NMETA:
                    e = min(end, NMETA)
                    out.append(("meta", c, c - col, e - c))
                elif c < PT:
                    e = min(end, PT)
                    out.append(("p", c - NMETA, c - col, e - c))
                else:
                    e = end
                    out.append(("s", c - PT, c - col, e - c))
                c = e
            return out

        in_src = {"meta": meta, "p": xp, "s": xs}
        out_dst = {"p": yp, "s": ys}

        stg_all = stg_in + stg_out
        r_stg_all = r_stgi + r_stgo
        sem_stg_all = sem_stgi + sem_stgo

        def load_x_dma_block(t, b):
            col0, N = TILES[t]
            rows = min(128, N - b * 128)
            si = b if t == 0 else rr_stgi.next()
            fns = []
            for kind, r0, poff, nr in segments(col0 + b * 128, rows):
                fns.append(lambda e, sm, kind=kind, r0=r0, poff=poff, nr=nr, si=si: e.dma_start(
                    out=stg_all[si][poff:poff + nr, :], in_=in_src[kind][r0:r0 + nr, :]).then_inc(sm, 16))
            P.dma("act" if (t == 0 or b >= 2) else "sp", fns, sem_stg_all[si], writes=[r_stg_all[si]])
            return (b, rows, si)

        def load_x_dma(t):
            col0, N = TILES[t]
            nblk = (N + 127) // 128
            return [load_x_dma_block(t, b) for b in range(min(4 if t == 0 else 2, nblk))]

        def load_x_transpose(t, blocks):
            col0, N = TILES[t]
            nblk = (N + 127) // 128
            blocks = list(blocks)
            bi = 0
            while bi < len(blocks):
                b, rows, si = blocks[bi]
                bi += 1
                for half in range(2):
                    bk = bank_f.next()
                    P.pe_group([(lambda e, cl=cl, bk=bk, rows=rows, si=si, half=half: e.transpose(
                        out=ps[bk][:, cl * 128:cl * 128 + rows],
                        in_=stg_all[si][0:rows, (half * 4 + cl) * 128:(half * 4 + cl + 1) * 128],
                        identity=ident32[0:rows, 0:rows]), [r_stg_all[si], r_ident32]) for cl in range(4)],
                        writes=[r_ps[bk]])
                    P.op("act", lambda e, bk=bk, rows=rows, half=half, b=b: e.activation(
                        out=xT[:, half * 4:(half + 1) * 4, b * 128:b * 128 + rows],
                        in_=ps[bk][:, :].rearrange("p (c n) -> p c n", n=128)[:, :, 0:rows], func=AF.Copy),
                        reads=[r_ps[bk]], writes=r_xT[half * 4:(half + 1) * 4])
                if len(blocks) < nblk:
                    blocks.append(load_x_dma_block(t, len(blocks)))

        def act_pre(func):
            P.op("act", lambda e: e.activation(out=dummy[:, 0:1], in_=epsc[:, 0:1], func=func), reads=[r_mhalf], writes=[r_dummy])

        def post_x(c, gcol, N):
            if gcol is not None:
                P.op("act", lambda e: e.activation(out=xn[:, c, 0:N], in_=xT[:, c, 0:N], func=AF.Identity, scale=cc(gcol + c)),
                     reads=[r_xT[c], r_consts], writes=[r_xn[c]])
            if c in (0, 7):
                P.op("act", lambda e: e.activation(out=sq_t[c][:, 0:N], in_=xT[:, c, 0:N], func=AF.Square),
                     reads=[r_xT[c]], writes=[r_sq[c]])
            elif c in (2, 3, 4):
                P.op("dve", lambda e: e.tensor_tensor(out=sq_t[c][:, 0:N], in0=xT[:, c, 0:N], in1=xT[:, c, 0:N], op=ALU.mult),
                     reads=[r_xT[c]], writes=[r_sq[c]])
            else:
                P.op("pool", lambda e: e.tensor_tensor(out=sq_t[c][:, 0:N], in0=xT[:, c, 0:N], in1=xT[:, c, 0:N], op=ALU.mult),
                     reads=[r_xT[c]], writes=[r_sq[c]])

        def stats_rstd(N):
            bk = bank_s.next()
            P.pe_group([(lambda e, c=c: e.matmul(ps[bk][:, 0:N], lhsT=ones_rms[:, :], rhs=sq_t[c][:, 0:N],
                                                 start=(c == 0), stop=(c == KC - 1)), [r_sq[c], r_ones]) for c in range(KC)],
                       writes=[r_ps[bk]])
            P.op("act", lambda e: e.activation(out=sst[:, 0:N], in_=ps[bk][:, 0:N], func=AF.Sqrt, bias=epsc[:, 0:1]),
                 reads=[r_ps[bk], r_mhalf], writes=[r_sst])
            act_pre(AF.Silu)
            P.op("dve", lambda e: e.reciprocal(out=rstd[:, 0:N], in_=sst[:, 0:N]), reads=[r_sst], writes=[r_rstd])

        def ffn(N, next_gcol):
            for j in range(JC):
                wt, wr = ring_gu.get()
                bg = bank_g.next()
                bu = bank_u.next()
                P.pe_group([(lambda e, kc=kc, wt=wt, bg=bg: e.matmul(ps[bg][:, 0:N], lhsT=wt[:, 0, kc, :], rhs=xn[:, kc, 0:N],
                                                                     start=(kc == 0), stop=(kc == KC - 1)), [wr, r_xn[kc]])
                            for kc in range(KC)], writes=[r_ps[bg]])
                P.pe_group([(lambda e, kc=kc, wt=wt, bu=bu: e.matmul(ps[bu][:, 0:N], lhsT=wt[:, 1, kc, :], rhs=xn[:, kc, 0:N],
                                                                     start=(kc == 0), stop=(kc == KC - 1)), [wr, r_xn[kc]])
                            for kc in range(KC)], writes=[r_ps[bu]])
                ring_gu.refill()
                si = rr_sg.next()
                P.op("dve", lambda e, si=si, bg=bg: e.tensor_tensor(out=sg_t[si][:, 0:N], in0=ps[bg][:, 0:N], in1=rstd[:, 0:N], op=ALU.mult),
                     reads=[r_ps[bg], r_rstd], writes=[r_sg[si]])
                P.op("act", lambda e, si=si: e.activation(out=sg_t[si][:, 0:N], in_=sg_t[si][:, 0:N], func=AF.Silu),
                     reads=[r_sg[si]], writes=[r_sg[si]])
                P.op("pool", lambda e, si=si: e.tensor_tensor(out=sg_t[si][:, 0:N], in0=sg_t[si][:, 0:N], in1=rstd[:, 0:N], op=ALU.mult),
                     reads=[r_sg[si], r_rstd], writes=[r_sg[si]])
                P.op("dve", lambda e, si=si, bu=bu, j=j: e.tensor_tensor(out=hbuf[:, j, 0:N], in0=ps[bu][:, 0:N],
                                                                         in1=sg_t[si][:, 0:N], op=ALU.mult),
                     reads=[r_ps[bu], r_sg[si]], writes=[r_h[j]])
                diag_some(2)
            act_pre(AF.Sqrt)
            for m in range(KC):
                wt, wr = ring_ds.get()
                bd = bank_d.next()
                P.pe_group([(lambda e, j=j, wt=wt, bd=bd: e.matmul(ps[bd][:, 0:N], lhsT=wt[:, j, :], rhs=hbuf[:, j, 0:N],
                                                                   start=(j == 0), stop=(j == JC - 1)), [wr, r_h[j]])
                            for j in range(JC)], writes=[r_ps[bd]])
                ring_ds.refill()
                P.op("dve", lambda e, m=m, bd=bd: e.scalar_tensor_tensor(
                    out=xT[:, m, 0:N], in0=ps[bd][:, 0:N], scalar=0.5, in1=xT[:, m, 0:N], op0=ALU.mult, op1=ALU.add),
                    reads=[r_ps[bd], r_xT[m]], writes=[r_xT[m]])
                if next_gcol is not None:
                    post_x(m, next_gcol, N)
                diag_some(10)
            diag_some(1000)
            if next_gcol is not None:
                stats_rstd(N)

        def mixer(t):
            col0, N = TILES[t]
            Np = min(N, PT - col0)
            Ns = N - Np
            first = (t == 0)
            last_p = (col0 + Np == PT)
            bm, bq = 6, 7
            st = {}

            def Z(c):
                wt, wr = ring_gu.get()
                ba = bank_g.next()
                bgt = bank_u.next()
                P.pe_group([(lambda e, kc=kc: e.matmul(ps[ba][:, 0:N], lhsT=wt[:, 0, kc, :], rhs=xn[:, kc, 0:N],
                                                       start=(kc == 0), stop=(kc == KC - 1)), [wr, r_xn[kc]])
                            for kc in range(KC)], writes=[r_ps[ba]])
                P.pe_group([(lambda e, kc=kc: e.matmul(ps[bgt][:, 0:N], lhsT=wt[:, 1, kc, :], rhs=xn[:, kc, 0:N],
                                                       start=(kc == 0), stop=(kc == KC - 1)), [wr, r_xn[kc]])
                            for kc in range(KC)], writes=[r_ps[bgt]])
                ring_gu.refill()
                ti = c % 2
                P.op("dve", lambda e: e.tensor_tensor(out=th_t[ti][:, 0:N], in0=ps[bgt][:, 0:N], in1=rstd[:, 0:N], op=ALU.mult),
                     reads=[r_ps[bgt], r_rstd], writes=[r_th[ti]])
                P.op("act", lambda e: e.activation(out=th_t[ti][:, 0:N], in_=th_t[ti][:, 0:N], func=AF.Tanh,
                                                   bias=hb[:, 4 + c:5 + c], scale=0.5), reads=[r_th[ti], r_hb], writes=[r_th[ti]])
                P.op("dve", lambda e: e.tensor_tensor(out=ah_t[ti][:, 0:N], in0=ps[ba][:, 0:N], in1=rstd[:, 0:N], op=ALU.mult),
                     reads=[r_ps[ba], r_rstd], writes=[r_ah[ti]])
                P.op("act", lambda e: e.activation(out=ah_t[ti][:, 0:N], in_=ah_t[ti][:, 0:N], func=AF.Identity,
                                                   bias=hb[:, c:c + 1], scale=0.5), reads=[r_ah[ti], r_hb], writes=[r_ah[ti]])
                P.op("dve", lambda e: e.scalar_tensor_tensor(
                    out=u32_t[ti][:, 0:N], in0=th_t[ti][:, 0:N], scalar=1.0, in1=ah_t[ti][:, 0:N], op0=ALU.add, op1=ALU.mult),
                    reads=[r_th[ti], r_ah[ti]], writes=[r_u32[ti]])
                P.op("act", lambda e: e.activation(out=ubf[:, c, CH:CH + Np], in_=u32_t[ti][:, 0:Np], func=AF.Copy),
                     reads=[r_u32[ti]], writes=[r_ubf[c]])
                if Ns:
                    P.op("act", lambda e: e.activation(
                        out=usbf[:, c, :, CH:CH + DSEQ], in_=u32_t[ti][:, Np:N].rearrange("p (s i) -> p s i", i=DSEQ),
                        func=AF.Copy), reads=[r_u32[ti]], writes=[r_usbf[c]])
                    P.op("dve", lambda e: e.tensor_copy(out=usnew[:, c, :], in_=u32_t[ti][:, Np:N]),
                         reads=[r_u32[ti]], writes=[r_usnew])
                if last_p:
                    P.op("dve", lambda e: e.tensor_copy(out=utail[:, c, :], in_=u32_t[ti][:, Np - CH:Np]),
                         reads=[r_u32[ti]], writes=[r_utail])

            def CONV(c):
                bc = bank_d.next()
                mm = [(lambda e, k=k: e.matmul(ps[bc][:, 0:Np], lhsT=diag[:, k * 4 + c, :], rhs=ubf[:, c, k:k + Np],
                                               start=(k == 0), stop=(k == CONV_W - 1)), [r_diag, r_ubf[c]])
                      for k in range(CONV_W)]
                if Ns:
                    mm += [(lambda e, k=k: e.matmul(
                        ps[bc][:, Np:N].rearrange("p (s i) -> p s i", i=DSEQ), lhsT=diag[:, k * 4 + c, :],
                        rhs=usbf[:, c, :, k:k + DSEQ], start=(k == 0), stop=(k == CONV_W - 1), skip_group_check=True),
                        [r_diag, r_usbf[c]]) for k in range(CONV_W)]
                P.pe_group(mm, writes=[r_ps[bc]])
                if not last_p:
                    P.op("act", lambda e: e.activation(out=ubf[:, c, 0:CH], in_=ubf[:, c, Np:Np + CH], func=AF.Copy),
                         reads=[r_ubf[c]], writes=[r_ubf[c]])
                bi = c % 2
                P.op("act", lambda e: e.activation(out=c32[:, c, 0:N], in_=ps[bc][:, 0:N], func=AF.Identity,
                                                   bias=cc(C_BDW + c)), reads=[r_ps[bc], r_consts], writes=[r_c32[c]])
                P.op("act", lambda e: e.activation(out=cbf_t[bi][:, 0:N], in_=ps[bc][:, 0:N], func=AF.Identity,
                                                   bias=cc(C_BDW + c)), reads=[r_ps[bc], r_consts], writes=[r_cbf[bi]])
                P.op("act", lambda e: e.activation(out=csq_t[bi][:, 0:N], in_=ps[bc][:, 0:N], func=AF.Square,
                                                   bias=cc(C_BDW + c)), reads=[r_ps[bc], r_consts], writes=[r_csq[bi]])

            def STAT(c):
                bi = c % 2
                P.pe_group([(lambda e: e.matmul(ps[bm][:, 0:N], lhsT=ones_ln[:, :], rhs=cbf_t[bi][:, 0:N],
                                                start=(c == 0), stop=(c == 3), skip_group_check=True), [r_cbf[bi], r_ones])],
                           writes=[r_ps[bm]] if c == 0 else [])
                st["m"] = ("pe", P.cnt["pe"])
                P.pe_group([(lambda e: e.matmul(ps[bq][:, 0:N], lhsT=ones_ln[:, :], rhs=csq_t[bi][:, 0:N],
                                                start=(c == 0), stop=(c == 3), skip_group_check=True), [r_csq[bi], r_ones])],
                           writes=[r_ps[bq]] if c == 0 else [])
                st["q"] = ("pe", P.cnt["pe"])
                if c == 3:
                    r_ps[bm].w = st["m"]
                    r_ps[bq].w = st["q"]

            def LN():
                P.op("act", lambda e: e.activation(out=ln_var[:, 0:N], in_=ps[bm][:, 0:N], func=AF.Square),
                     reads=[r_ps[bm]], writes=[r_lnv])
                P.op("dve", lambda e: e.scalar_tensor_tensor(out=ln_var[:, 0:N], in0=ps[bq][:, 0:N], scalar=EPS, in1=ln_var[:, 0:N],
                                                             op0=ALU.add, op1=ALU.subtract), reads=[r_ps[bq], r_lnv], writes=[r_lnv])
                P.op("act", lambda e: e.activation(out=ln_var[:, 0:N], in_=ln_var[:, 0:N], func=AF.Sqrt), reads=[r_lnv], writes=[r_lnv])
                act_pre(AF.Silu)

                def center(c):
                    P.op("dve", lambda e: e.tensor_tensor(out=c32[:, c, 0:N], in0=c32[:, c, 0:N], in1=ps[bm][:, 0:N], op=ALU.subtract),
                         reads=[r_c32[c], r_ps[bm]], writes=[r_c32[c]])

                def scale(c):
                    P.op("dve", lambda e: e.tensor_tensor(out=c32[:, c, 0:N], in0=c32[:, c, 0:N], in1=ln_rstd[:, 0:N], op=ALU.mult),
                         reads=[r_c32[c], r_lnr], writes=[r_c32[c]])
                    P.op("act", lambda e: e.activation(out=cq[:, c, 0:N], in_=c32[:, c, 0:N], func=AF.Silu,
                                                       bias=cc(C_LNB + c), scale=cc(C_LNG + c)),
                         reads=[r_c32[c], r_consts], writes=[r_cq[c]])

                center(0)
                P.op("dve", lambda e: e.reciprocal(out=ln_rstd[:, 0:N], in_=ln_var[:, 0:N]), reads=[r_lnv], writes=[r_lnr])
                scale(0)
                for c in range(1, 4):
                    center(c)
                    scale(c)

            def PZ(pair):
                wt, wr = ring_gu.get()
                for a in range(2):
                    g = pair * 2 + a
                    w = WINS[g]
                    if pair == 0:
                        bp = 6 if a == 0 else 7
                    else:
                        bp = bank_g.next() if a == 0 else bank_u.next()
                    P.pe_group([(lambda e, kc=kc, a=a, bp=bp: e.matmul(ps[bp][:, 0:N], lhsT=wt[:, a, kc, :], rhs=xn[:, kc, 0:N],
                                                                       start=(kc == 0), stop=(kc == KC - 1)), [wr, r_xn[kc]])
                                for kc in range(KC)], writes=[r_ps[bp]])
                    if a == 1:
                        ring_gu.refill()
                    P.op("dve", lambda e, g=g, bp=bp: e.tensor_tensor(out=pext[:, g, PH:PH + Np], in0=ps[bp][:, 0:Np], in1=rstd[:, 0:Np],
                                                                        op=ALU.mult), reads=[r_ps[bp], r_rstd], writes=[r_pext[g]])
                    P.op("act", lambda e, g=g: e.activation(out=pext[:, g, PH:PH + Np], in_=pext[:, g, PH:PH + Np], func=AF.Identity,
                                                            bias=cc(C_BIN + 8 + g)), reads=[r_pext[g], r_consts], writes=[r_pext[g]])
                    if Ns:
                        P.op("dve", lambda e, g=g, bp=bp: e.tensor_tensor(
                            out=psext[:, g, :, PH:PH + DSEQ], in0=ps[bp][:, Np:N].rearrange("p (s i) -> p s i", i=DSEQ),
                            in1=rstd[:, Np:N].rearrange("p (s i) -> p s i", i=DSEQ), op=ALU.mult),
                            reads=[r_ps[bp], r_rstd], writes=[r_psext[g]])
                        P.op("act", lambda e, g=g: e.activation(
                            out=psext[:, g, :, PH:PH + DSEQ], in_=psext[:, g, :, PH:PH + DSEQ],
                            func=AF.Identity, bias=cc(C_BIN + 8 + g)), reads=[r_psext[g], r_consts], writes=[r_psext[g]])
                    di = g
                    L = PH + Np
                    src, rsrc = pext[:, g, 0:L], r_pext[g]
                    bufs = [(sA, r_sA), (sB, r_sB)]
                    step = 1
                    bi = 0
                    while step < w:
                        dst, rdst = bufs[bi]
                        P.op("pool", lambda e, src=src, dst=dst, step=step, L=L: e.tensor_tensor(
                            out=dst[:, step:L], in0=src[:, step:L], in1=src[:, 0:L - step], op=ALU.add),
                            reads=[rsrc], writes=[rdst])
                        src, rsrc = dst[:, 0:L], rdst
                        step *= 2
                        bi ^= 1
                    P.op("dve", lambda e, src=src, g=g, di=di, w=w: e.scalar_tensor_tensor(
                        out=dd_t[di][:, 0:Np], in0=src[:, PH:PH + Np], scalar=1.0 / w, in1=pext[:, g, PH:PH + Np],
                        op0=ALU.mult, op1=ALU.subtract), reads=[rsrc, r_pext[g]], writes=[r_dd[di]])
                    if first:
                        P.op("dve", lambda e, src=src, g=g: e.tensor_tensor(
                            out=sst[:, 0:16], in0=src[:, PH:PH + 16], in1=invc[:, g * 16:(g + 1) * 16], op=ALU.mult),
                            reads=[rsrc, r_invc], writes=[r_sst])
                        P.op("dve", lambda e, g=g, di=di: e.tensor_tensor(
                            out=dd_t[di][:, 0:16], in0=sst[:, 0:16], in1=pext[:, g, PH:PH + 16], op=ALU.subtract),
                            reads=[r_sst, r_pext[g]], writes=[r_dd[di]])
                    if Ns:
                        Ls = PH + DSEQ
                        src2, rsrc2 = psext[:, g, :, :], r_psext[g]
                        bufs2 = [(sAs, r_sAs), (sBs, r_sBs)]
                        step = 1
                        bi = 0
                        while step < w:
                            dst2, rdst2 = bufs2[bi]
                            P.op("pool", lambda e, src2=src2, dst2=dst2, step=step, Ls=Ls: e.tensor_tensor(
                                out=dst2[:, :, step:Ls], in0=src2[:, :, step:Ls], in1=src2[:, :, 0:Ls - step], op=ALU.add),
                                reads=[rsrc2], writes=[rdst2])
                            src2, rsrc2 = dst2[:, :, :], rdst2
                            step *= 2
                            bi ^= 1
                        P.op("dve", lambda e, src2=src2, g=g, di=di, w=w: e.scalar_tensor_tensor(
                            out=dd_t[di][:, Np:N].rearrange("p (s i) -> p s i", i=DSEQ), in0=src2[:, :, PH:PH + DSEQ],
                            scalar=1.0 / w, in1=psext[:, g, :, PH:PH + DSEQ], op0=ALU.mult, op1=ALU.subtract),
                            reads=[rsrc2, r_psext[g]], writes=[r_dd[di]])
                        P.op("dve", lambda e, g=g: e.tensor_copy(
                            out=psnew[:, g, :].rearrange("p (s i) -> p s i", i=DSEQ), in_=psext[:, g, :, PH:PH + DSEQ]),
                            reads=[r_psext[g]], writes=[r_psnew])
                    if last_p:
                        P.op("dve", lambda e, g=g: e.tensor_copy(out=ptail[:, g, :], in_=pext[:, g, Np:Np + PH]),
                             reads=[r_pext[g]], writes=[r_ptail])
                    else:
                        P.op("pool", lambda e, g=g: e.tensor_copy(out=pext[:, g, 0:PH], in_=pext[:, g, Np:Np + PH]),
                             reads=[r_pext[g]], writes=[r_pext[g]])

            def Q(g):
                bqq = bank_d.next()
                P.pe_group([(lambda e: e.matmul(ps[bqq][:, 0:N], lhsT=wpool_b[:, g, :], rhs=dd_t[g][:, 0:N],
                                                start=True, stop=True), [r_wpb, r_dd[g]])], writes=[r_ps[bqq]])
                P.op("act", lambda e: e.activation(out=cq[:, 4 + g, 0:N], in_=ps[bqq][:, 0:N], func=AF.Identity,
                                                   scale=cc(C_PSC + g)), reads=[r_ps[bqq], r_consts], writes=[r_cq[4 + g]])

            def OUT():
                for pair in range(4):
                    wt, wr = ring_gu.get()
                    for a in range(2):
                        m = pair * 2 + a
                        bo = bank_d.next()
                        P.pe_group([(lambda e, k=k, a=a, bo=bo, wt=wt: e.matmul(ps[bo][:, 0:N], lhsT=wt[:, a, k, :], rhs=cq[:, k, 0:N],
                                                                         start=(k == 0), stop=(k == 7)), [wr, r_cq[k]])
                                    for k in range(8)], writes=[r_ps[bo]])
                        if a == 1:
                            ring_gu.refill()
                        P.op("dve", lambda e, m=m, bo=bo: e.scalar_tensor_tensor(
                            out=xT[:, m, 0:N], in0=ps[bo][:, 0:N], scalar=cc(C_BOUT + m), in1=xT[:, m, 0:N], op0=ALU.add, op1=ALU.add),
                            reads=[r_ps[bo], r_xT[m], r_consts], writes=[r_xT[m]])
                        post_x(m, C_G2, N)
                stats_rstd(N)

            Z(0)
            Z(1)
            PZ(0)
            CONV(0)
            Z(2)
            STAT(0)
            PZ(1)
            CONV(1)
            Z(3)
            act_pre(AF.Sqrt)
            STAT(1)
            CONV(2)
            for g in range(4):
                Q(g)
            STAT(2)
            CONV(3)
            STAT(3)
            LN()
            act_pre(AF.Sqrt)
            fb = bank_g.next()
            P.pe_group([(lambda e: e.matmul(ps[fb][:, 0:N], lhsT=ones_ln[:, :], rhs=xn[:, 0, 0:N], start=True, stop=True),
                         [r_ones, r_xn[0]]) for _ in range(24)], writes=[r_ps[fb]])
            OUT()

        out_toks = []


        def final(t):
            col0, N = TILES[t]
            nblk = (N + 127) // 128
            for b in range(nblk):
                rows = min(128, N - b * 128)
                so = rr_stgo.next()
                bks = []
                for half in range(2):
                    bk = bank_f.next()
                    bks.append(bk)
                    P.pe_group([(lambda e, cl=cl, bk=bk, rows=rows, half=half, b=b: e.transpose(
                        out=ps[bk][0:rows, cl * 128:(cl + 1) * 128], in_=xT[:, half * 4 + cl, b * 128:b * 128 + rows],
                        identity=ident32[:, :]), [r_xT[half * 4 + cl], r_ident32]) for cl in range(4)], writes=[r_ps[bk]])
                    P.op("act", lambda e, bk=bk, rows=rows, half=half, so=so: e.activation(
                        out=stg_st[0:rows, :], in_=ps[bk][0:rows, :], func=AF.Square, accum_out=rcol[so][0:rows, half:half + 1]),
                        reads=[r_ps[bk]], writes=[r_stgst, r_rcol[so]])
                P.op("dve", lambda e, rows=rows, so=so: e.tensor_tensor(out=rcol[so][0:rows, 2:3], in0=rcol[so][0:rows, 0:1],
                                                                        in1=rcol[so][0:rows, 1:2], op=ALU.add),
                     reads=[r_rcol[so]], writes=[r_rcol[so]])
                P.op("act", lambda e, rows=rows, so=so: e.activation(out=rcol[so][0:rows, 3:4], in_=rcol[so][0:rows, 2:3], func=AF.Sqrt,
                                                                     bias=epsc[0:rows, 0:1], scale=1.0 / D),
                     reads=[r_rcol[so], r_mhalf], writes=[r_rcol[so]])
                P.op("dve", lambda e, rows=rows, so=so: e.reciprocal(out=rcol[so][0:rows, 4:5], in_=rcol[so][0:rows, 3:4]),
                     reads=[r_rcol[so]], writes=[r_rcol[so]])
                for half in range(2):
                    bk = bks[half]
                    P.op("dve", lambda e, bk=bk, rows=rows, half=half, so=so: e.scalar_tensor_tensor(
                        out=stg_out[so][0:rows, half * 512:(half + 1) * 512], in0=ps[bk][0:rows, :], scalar=rcol[so][0:rows, 4:5],
                        in1=gf_bc[0:rows, half * 512:(half + 1) * 512], op0=ALU.mult, op1=ALU.mult),
                        reads=[r_ps[bk], r_rcol[so], r_consts], writes=[r_stgo[so]])
                fns = []
                for kind, r0, poff, nr in segments(col0 + b * 128, rows):
                    if kind == "meta":
                        continue
                    fns.append(lambda e, sm, kind=kind, r0=r0, poff=poff, nr=nr, so=so: e.dma_start(
                        out=out_dst[kind][r0:r0 + nr, :], in_=stg_out[so][poff:poff + nr, :]).then_inc(sm, 16))
                if fns:
                    P.dma("sp", fns, sem_stgo[so], reads=[r_stgo[so]])

        def state_outputs():
            for (src, rs, rows, dst) in ((utail, r_utail, CH, ncp), (ptail, r_ptail, PH, npp)):
                bk = bank_t.next()
                P.pe_group([(lambda e, c=c, bk=bk, src=src, rows=rows: e.transpose(
                    out=ps[bk][0:rows, c * 128:(c + 1) * 128], in_=src[:, c, :], identity=ident32[:, :]), [rs, r_ident32])
                    for c in range(4)], writes=[r_ps[bk]])
                so = rr_so.next()
                P.op("act", lambda e, bk=bk, rows=rows, so=so: e.activation(out=stg_so[so][0:rows, :], in_=ps[bk][0:rows, :], func=AF.Copy),
                     reads=[r_ps[bk]], writes=[r_stgso[so]])
                P.dma("sp", [lambda e, sm, so=so, rows=rows, dst=dst: e.dma_start(out=dst[:, :], in_=stg_so[so][0:rows, :]).then_inc(sm, 16)],
                      sem_so[so], reads=[r_stgso[so]])
            for (src, rs, hist, dst) in ((usnew, r_usnew, CH, ncs), (psnew, r_psnew, PH, nps)):
                bk = bank_t.next()
                P.pe_group([(lambda e, c=c, bk=bk, src=src: e.transpose(
                    out=ps[bk][0:ST, c * 128:(c + 1) * 128], in_=src[:, c, :], identity=ident32[:, :]), [rs, r_ident32])
                    for c in range(4)], writes=[r_ps[bk]])
                so = rr_so.next()
                P.op("act", lambda e, bk=bk, so=so: e.activation(out=stg_so[so][0:ST, :], in_=ps[bk][0:ST, :], func=AF.Copy),
                     reads=[r_ps[bk]], writes=[r_stgso[so]])
                fns = []
                for s in range(NSS):
                    r0 = s * hist + hist - DSEQ
                    fns.append(lambda e, sm, s=s, r0=r0, so=so, dst=dst: e.dma_start(
                        out=dst[r0:r0 + DSEQ, :], in_=stg_so[so][s * DSEQ:(s + 1) * DSEQ, :]).then_inc(sm, 16))
                P.dma("sp", fns, sem_so[so], reads=[r_stgso[so]])

        blocks = load_x_dma(0)
        q0, f0, rd0, p0 = gu_loads[0]
        gu_loads[0] = (q0, f0, list(rd0) + [r_consts, r_stgi[0]], p0)
        ring_gu.refill()
        ring_ds.refill()
        for t in range(len(TILES)):
            col0, N = TILES[t]
            load_x_transpose(t, blocks)
            dump(t, 0)
            for c in range(KC):
                post_x(c, C_G1, N)
            stats_rstd(N)
            ffn(N, C_GM)
            if t == 0:
                load_states()
            dump(t, 1)
            mixer(t)
            if t == len(TILES) - 1:
                state_outputs()
            dump(t, 2)
            if t + 1 < len(TILES):
                blocks = load_x_dma(t + 1)
            ffn(N, None)
            dump(t, 3)
            final(t)
            dump(t, 4)
        assert ring_gu.consumed == len(gu_loads) and ring_ds.consumed == len(ds_loads)

        sem_names = list(Prog.QUEUES[:4]) + P.dma_sems
        sem_ctx = {}
        for name in sem_names:
            sem_ctx[name] = es.enter_context(nc.semaphore(name))
        final_waits = [(s, P.cnt[s]) for s in P.dma_sems if P.cnt[s] > 0 and (s.startswith("stgo") or s.startswith("stgi") or s.startswith("so") or s == "d2d" or s == "dbg")]
        block = es.enter_context(nc.Block())

        @block.tensor
        def _(e):
            P.replay("pe", e, sem_ctx)

        @block.scalar
        def _(e):
            P.replay("act", e, sem_ctx)

        @block.vector
        def _(e):
            P.replay("dve", e, sem_ctx)

        @block.gpsimd
        def _(e):
            P.replay("pool", e, sem_ctx)

        @block.sync
        def _(e):
            P.replay("sp", e, sem_ctx)
            for s, v in final_waits:
                e.wait_ge(sem_ctx[s], v)
    return nc


_CACHE = {}


def _consts_array(inp):
    c = np.zeros((128, NCONST), np.float32)

    def put(col, vec):
        v = np.asarray(vec, np.float32).reshape(-1, 128)
        c[:, col:col + v.shape[0]] = v.T

    put(C_G1, inp["norm_ffn1"][0])
    put(C_GM, inp["norm_mix"][0])
    put(C_G2, inp["norm_ffn2"][0])
    put(C_GF, inp["norm_final"])
    put(C_BIN, inp["b_in"][0])
    put(C_BDW, inp["b_dw"][0])
    put(C_LNG, inp["ln_conv_g"][0])
    put(C_LNB, inp["ln_conv_b"][0])
    put(C_PSC, inp["pool_scale"][0])
    put(C_BOUT, inp["b_out"][0])
    wdw = np.asarray(inp["w_dw"][0], np.float32)
    for k in range(CONV_W):
        put(C_WDW + k * 4, wdw[k])
    return c


def _gu_layout(wa, wb, cols_a, cols_b):
    n = len(cols_a)
    out = np.empty((n, 128, 2, KC, 128), np.float32)
    wa4 = wa.reshape(KC, 128, -1, 128)
    wb4 = wb.reshape(KC, 128, -1, 128)
    for i in range(n):
        out[i, :, 0] = wa4[:, :, cols_a[i], :].transpose(1, 0, 2)
        out[i, :, 1] = wb4[:, :, cols_b[i], :].transpose(1, 0, 2)
    return out.reshape(n, 128, 2 * KC * 128)


def _d_layout(wdn):
    w4 = wdn.reshape(JC, 128, KC, 128)
    return np.ascontiguousarray(w4.transpose(2, 1, 0, 3)).reshape(KC, 128, JC * 128)


def kernel(**inp):
    inp = {k: np.asarray(v) for k, v in inp.items()}
    if "nc" not in _CACHE:
        _CACHE["nc"] = build_program()
    nc = _CACHE["nc"]

    w1g, w1u, w1d = inp["w_ffn1_gate"][0], inp["w_ffn1_up"][0], inp["w_ffn1_down"][0]
    w2g, w2u, w2d = inp["w_ffn2_gate"][0], inp["w_ffn2_up"][0], inp["w_ffn2_down"][0]
    w_in, w_out = inp["w_in"][0], inp["w_out"][0]
    shared = {
        "wgu1": _gu_layout(w1g, w1u, list(range(JC)), list(range(JC))),
        "wgu2": _gu_layout(w2g, w2u, list(range(JC)), list(range(JC))),
        "wd1": _d_layout(w1d),
        "wd2": _d_layout(w2d),
        "win": _gu_layout(w_in, w_in, [0, 1, 2, 3, 8, 10], [4, 5, 6, 7, 9, 11]),
        "wout": _gu_layout(w_out, w_out, [0, 2, 4, 6], [1, 3, 5, 7]),
        "wpool": np.ascontiguousarray(inp["w_pool"][0].transpose(1, 0, 2)).reshape(128, 512).astype(np.float32),
        "consts": _consts_array(inp),
        "ident": np.eye(128, dtype=np.float32),
        "meta": np.ascontiguousarray(inp["meta_tokens"], np.float32),
    }
    invc = np.zeros((128, 64), np.float32)
    for g, w in enumerate(WINS):
        invc[:, g * 16:(g + 1) * 16] = 1.0 / np.minimum(np.arange(16) + 1, w).astype(np.float32)
    shared["invc"] = invc
    shared["gfbc"] = np.ascontiguousarray(np.broadcast_to(np.asarray(inp["norm_final"], np.float32)[None, :], (128, D)))

    in_maps = []
    for c in range(8):
        m = dict(shared)
        m["xp"] = np.ascontiguousarray(inp["x_prompt"][c], np.float32)
        m["xs"] = np.ascontiguousarray(inp["x_sample"][c * NSS:(c + 1) * NSS].reshape(ST, D), np.float32)
        m["sconv"] = np.ascontiguousarray(inp["state_conv"][0, c * NSS:(c + 1) * NSS].reshape(NSS * CH, CC), np.float32)
        m["spool"] = np.ascontiguousarray(inp["state_pool"][0, c * NSS:(c + 1) * NSS].reshape(NSS * PH, CC), np.float32)
        in_maps.append(m)
    res = run_bass_kernel_spmd(nc, in_maps, core_ids=list(range(8)))
    rs = res.results
    _CACHE["last"] = rs
    y_prompt = np.stack([rs[c]["yp"] for c in range(8)], 0)
    y_sample = np.concatenate([rs[c]["ys"].reshape(NSS, DSEQ, D) for c in range(8)], 0)
    ncp_o = np.stack([rs[c]["ncp"] for c in range(8)], 0)[None]
    npp_o = np.stack([rs[c]["npp"] for c in range(8)], 0)[None]
    ncs_o = np.concatenate([rs[c]["ncs"].reshape(NSS, CH, CC) for c in range(8)], 0)[None]
    nps_o = np.concatenate([rs[c]["nps"].reshape(NSS, PH, CC) for c in range(8)], 0)[None]
    return (y_prompt.astype(np.float32), y_sample.astype(np.float32), ncp_o.astype(np.float32),
            npp_o.astype(np.float32), ncs_o.astype(np.float32), nps_o.astype(np.float32))
```
